# Optimizing a Trainium2 kernel written in Bass

```python
import jax, jax.numpy as jnp
from jax import lax
import numpy as np

D_MODEL = 1024
BATCH = 8
SEQ = 2048
DEPTH = 4

N_META = 16
D_FF = 2816
RES_HALF = 0.5
EPS = 1e-6
SB_HEADS = 16
SB_HEAD_DIM = D_MODEL // SB_HEADS
SB_BLOCK = 128
GLA_HEADS = 4
GLA_DK = D_MODEL // 2
GLA_DV = D_MODEL
GLA_HK = GLA_DK // GLA_HEADS
GLA_HV = GLA_DV // GLA_HEADS
GLA_GATE_RANK = 16
GLA_TAU = 16.0
GLA_CHUNK = 64
GLA_IN = 2 * GLA_DK + 2 * GLA_DV + GLA_GATE_RANK
N_SB = (DEPTH + 1) // 2
N_GLA = DEPTH // 2

kernel_name = "hybrid_stickbreak_gla_macaron"


def _rmsnorm(x, g):
    xf = x.astype(jnp.float32)
    y = xf * lax.rsqrt(jnp.mean(xf * xf, axis=-1, keepdims=True) + EPS)
    return (y * g.astype(jnp.float32)).astype(x.dtype)


def _swiglu(h, w_gu, w_down):
    g, u = jnp.split(h @ w_gu, 2, axis=-1)
    return (jax.nn.silu(g) * u) @ w_down


def _stick_breaking(h, w_qkv, g_q, g_k, w_o):
    B, L, _ = h.shape
    qkv = (h @ w_qkv).reshape(B, L, 3, SB_HEADS, SB_HEAD_DIM)
    q = _rmsnorm(qkv[:, :, 0], g_q)
    k = _rmsnorm(qkv[:, :, 1], g_k)
    v = qkv[:, :, 2]
    pad = (-L) % SB_BLOCK
    def prep(t):
        return jnp.pad(t, ((0, 0), (pad, 0), (0, 0), (0, 0))).transpose(0, 2, 1, 3)
    q, k, v = prep(q), prep(k), prep(v)
    Lp = L + pad
    scale = SB_HEAD_DIM ** -0.5
    outs = []
    for t0 in range(0, Lp, SB_BLOCK):
        t1 = t0 + SB_BLOCK
        z = jnp.einsum('bhtd,bhsd->bhts', q[:, :, t0:t1], k[:, :, :t1]).astype(jnp.float32) * scale
        t_pos = jnp.arange(t0, t1)[:, None]
        s_pos = jnp.arange(t1)[None, :]
        mask = (s_pos < t_pos) & (s_pos >= pad)
        log_1m = jnp.where(mask, jax.nn.log_sigmoid(-z), 0.0)
        log_after = lax.cumsum(log_1m, axis=3, reverse=True) - log_1m
        w = jnp.where(mask, jnp.exp(jax.nn.log_sigmoid(z) + log_after), 0.0)
        outs.append(jnp.einsum('bhts,bhsd->bhtd', w.astype(v.dtype), v[:, :, :t1]))
    o = jnp.concatenate(outs, axis=2)[:, :, pad:]
    o = o.transpose(0, 2, 1, 3).reshape(B, L, SB_HEADS * SB_HEAD_DIM)
    return o @ w_o


def _gla(h, w_in, w_gate_up, b_gate, g_out, w_o):
    B, L, _ = h.shape
    f32 = jnp.float32
    proj = h @ w_in
    q, k, v, r, g_low = jnp.split(
        proj, [GLA_DK, 2 * GLA_DK, 2 * GLA_DK + GLA_DV, 2 * GLA_DK + 2 * GLA_DV], axis=-1)
    log_a = jax.nn.log_sigmoid((g_low @ w_gate_up + b_gate).astype(f32)) / GLA_TAU
    pad = (-L) % GLA_CHUNK
    Lp = L + pad
    N = Lp // GLA_CHUNK
    def chunk(t, hd):
        t = jnp.pad(t, ((0, 0), (pad, 0), (0, 0)))
        return t.reshape(B, N, GLA_CHUNK, GLA_HEADS, hd).transpose(0, 3, 1, 2, 4)
    qc = chunk(q.astype(f32) * GLA_HK ** -0.5, GLA_HK)
    kc = chunk(k.astype(f32), GLA_HK)
    vc = chunk(v.astype(f32), GLA_HV)
    ac = chunk(log_a, GLA_HK)
    b = jnp.cumsum(ac, axis=3)
    b_last = b[:, :, :, -1:, :]
    q_dec = qc * jnp.exp(b)
    k_dec = kc * jnp.exp(-b)
    k_st = kc * jnp.exp(b_last - b)
    causal = jnp.tril(jnp.ones((GLA_CHUNK, GLA_CHUNK), dtype=bool))
    att = jnp.where(causal, jnp.einsum('bhnck,bhnsk->bhncs', q_dec, k_dec), 0.0)
    o_intra = jnp.einsum('bhncs,bhnsv->bhncv', att, vc)

    def step(S, xs):
        q_n, k_n, v_n, dec_n = xs
        o_n = jnp.einsum('bhck,bhkv->bhcv', q_n, S)
        S = S * jnp.swapaxes(dec_n, -1, -2) + jnp.einsum('bhck,bhcv->bhkv', k_n, v_n)
        return S, o_n

    xs = tuple(jnp.moveaxis(t, 2, 0) for t in (q_dec, k_st, vc, jnp.exp(b_last)))
    S0 = jnp.zeros((B, GLA_HEADS, GLA_HK, GLA_HV), f32)
    _, o_inter = lax.scan(step, S0, xs)
    o = o_intra + jnp.moveaxis(o_inter, 0, 2)
    o = o.transpose(0, 2, 3, 1, 4).reshape(B, Lp, GLA_HEADS, GLA_HV)[:, pad:]
    o = o * lax.rsqrt(jnp.mean(o * o, axis=-1, keepdims=True) + EPS)
    o = o * g_out.astype(f32).reshape(GLA_HEADS, GLA_HV)
    o = o.reshape(B, L, GLA_DV) * jax.nn.silu(r.astype(f32))
    return o.astype(h.dtype) @ w_o


def setup_inputs(seed: int = 0) -> dict:
    key = jax.random.key(seed)
    ks = jax.random.split(key, 20)
    f32 = jnp.float32
    def dense(k, shape, fan_in):
        return jax.random.normal(k, shape, f32) * fan_in ** -0.5
    def gain(k, shape):
        return 1.0 + 0.02 * jax.random.normal(k, shape, f32)
    return {
        "x": jax.random.normal(ks[0], (BATCH, SEQ, D_MODEL), f32),
        "meta": jax.random.normal(ks[1], (N_META, D_MODEL), f32),
        "ffn_a_norm": gain(ks[2], (DEPTH, D_MODEL)),
        "ffn_a_w_gu": dense(ks[3], (DEPTH, D_MODEL, 2 * D_FF), D_MODEL),
        "ffn_a_w_down": dense(ks[4], (DEPTH, D_FF, D_MODEL), D_FF),
        "mix_norm": gain(ks[5], (DEPTH, D_MODEL)),
        "sb_w_qkv": dense(ks[6], (N_SB, D_MODEL, 3 * D_MODEL), D_MODEL),
        "sb_q_norm": gain(ks[7], (N_SB, SB_HEAD_DIM)),
        "sb_k_norm": gain(ks[8], (N_SB, SB_HEAD_DIM)),
        "sb_w_o": dense(ks[9], (N_SB, D_MODEL, D_MODEL), D_MODEL),
        "gla_w_in": dense(ks[10], (N_GLA, D_MODEL, GLA_IN), D_MODEL),
        "gla_w_gate_up": dense(ks[11], (N_GLA, GLA_GATE_RANK, GLA_DK), GLA_GATE_RANK),
        "gla_b_gate": 0.1 * jax.random.normal(ks[12], (N_GLA, GLA_DK), f32),
        "gla_out_norm": gain(ks[13], (N_GLA, GLA_DV)),
        "gla_w_o": dense(ks[14], (N_GLA, GLA_DV, D_MODEL), GLA_DV),
        "ffn_b_norm": gain(ks[15], (DEPTH, D_MODEL)),
        "ffn_b_w_gu": dense(ks[16], (DEPTH, D_MODEL, 2 * D_FF), D_MODEL),
        "ffn_b_w_down": dense(ks[17], (DEPTH, D_FF, D_MODEL), D_FF),
    }


def reference(x, meta, ffn_a_norm, ffn_a_w_gu, ffn_a_w_down, mix_norm,
              sb_w_qkv, sb_q_norm, sb_k_norm, sb_w_o,
              gla_w_in, gla_w_gate_up, gla_b_gate, gla_out_norm, gla_w_o,
              ffn_b_norm, ffn_b_w_gu, ffn_b_w_down):
    B = x.shape[0]
    m = jnp.broadcast_to(meta.astype(x.dtype)[None], (B, N_META, x.shape[-1]))
    h = jnp.concatenate([m, x], axis=1)
    for i in range(DEPTH):
        h = h + RES_HALF * _swiglu(_rmsnorm(h, ffn_a_norm[i]), ffn_a_w_gu[i], ffn_a_w_down[i])
        hn = _rmsnorm(h, mix_norm[i])
        j = i // 2
        if i % 2 == 0:
            h = h + _stick_breaking(hn, sb_w_qkv[j], sb_q_norm[j], sb_k_norm[j], sb_w_o[j])
        else:
            h = h + _gla(hn, gla_w_in[j], gla_w_gate_up[j], gla_b_gate[j], gla_out_norm[j], gla_w_o[j])
        h = h + RES_HALF * _swiglu(_rmsnorm(h, ffn_b_norm[i]), ffn_b_w_gu[i], ffn_b_w_down[i])
    return h[:, N_META:]
```

```python
from contextlib import ExitStack

import numpy as np
import concourse.bass as bass
import concourse.mybir as mybir
from concourse.bass_utils import run_bass_kernel_spmd

F32 = mybir.dt.float32
BF16 = mybir.dt.bfloat16
AF = mybir.ActivationFunctionType
ALU = mybir.AluOpType

D = 1024
SEQ = 2048
NMETA = 16
T = SEQ + NMETA
DFF = 2816
NFC = 22
DEPTH = 4
EPS = 1e-6
TT = [(0, 512), (512, 512), (1024, 512), (1536, 512), (2048, 16)]
TOKT = [(i * 128, 128) for i in range(16)] + [(2048, 16)]
NEG = -30000.0


class Buf:
    __slots__ = ("w", "r")

    def __init__(self):
        self.w = None
        self.r = {}


class Sched:
    def __init__(self, nc, st):
        self.nc = nc
        self.st = st
        self.E = {"pe": nc.tensor, "act": nc.scalar, "dve": nc.vector, "pool": nc.gpsimd, "sp": nc.sync}
        self.sems = {}
        self.cnt = {}
        for k in ("pe", "act", "dve", "pool"):
            self.sems[k] = st.enter_context(nc.semaphore("s_" + k))
            self.cnt[k] = 0
        self.seen = {e: {} for e in self.E}
        self.ndma = 0

    def dma_sem(self):
        k = "d%d" % self.ndma
        self.ndma += 1
        self.sems[k] = self.st.enter_context(self.nc.semaphore("s_" + k))
        self.cnt[k] = 0
        return k

    def wait(self, eng, toks):
        need = {}
        for t in toks:
            if t is None:
                continue
            k, v = t
            if eng == "pe" and k == "pe":
                continue
            if self.seen[eng].get(k, 0) >= v:
                continue
            if need.get(k, 0) < v:
                need[k] = v
        for k, v in need.items():
            self.E[eng].wait_ge(self.sems[k], v)
            self.seen[eng][k] = v

    @staticmethod
    def deps(reads, writes):
        toks = []
        for b in reads:
            toks.append(b.w)
        for b in writes:
            toks.append(b.w)
            toks.extend(b.r.items())
        return toks

    def _mark(self, tok, reads, writes):
        k, v = tok
        for b in reads:
            if b.r.get(k, 0) < v:
                b.r[k] = v
        for b in writes:
            b.w = tok
            b.r = {}

    def op(self, eng, fn, reads=(), writes=(), inc=True):
        self.wait(eng, self.deps(reads, writes))
        ins = fn()
        if inc:
            ins.then_inc(self.sems[eng], 1)
            self.cnt[eng] += 1
            tok = (eng, self.cnt[eng])
        else:
            tok = (eng, self.cnt[eng] + 1)
        self._mark(tok, reads, writes)
        return tok

    def dma(self, eng, out, in_, semk, reads=(), writes=()):
        self.wait(eng, self.deps(reads, writes))
        self.E[eng].dma_start(out=out, in_=in_).then_inc(self.sems[semk], 16)
        self.cnt[semk] += 16
        tok = (semk, self.cnt[semk])
        self._mark(tok, reads, writes)
        return tok

    def barrier(self):
        toks = [(k, v) for k, v in self.cnt.items() if v > 0]
        for e in self.E:
            self.wait(e, toks)


def build_program(n_sub=3 * DEPTH, dbg=False, gla_steps=4):
    nc = bass.Bass("TRN2", target_bir_lowering=False)
    dt = nc.dram_tensor
    x_d = dt("x", [SEQ, D], F32, kind="ExternalInput").ap()
    meta_d = dt("meta", [NMETA, D], F32, kind="ExternalInput").ap()
    wgu_d = dt("wgu", [2 * DEPTH, NFC, 128, 8 * 256], F32, kind="ExternalInput").ap()
    wd_d = dt("wd", [2 * DEPTH, 2, 8, 128, 11 * 128], F32, kind="ExternalInput").ap()
    wqkv_d = dt("wqkv", [2, 4, 128, 8 * 768], F32, kind="ExternalInput").ap()
    wosb_d = dt("wosb", [2, 4, 128, 2 * 1024], F32, kind="ExternalInput").ap()
    win_d = dt("win", [2, 4, 128, 8 * 512], F32, kind="ExternalInput").ap()
    winr_d = dt("winr", [2, 4, 128, 8 * 256], F32, kind="ExternalInput").ap()
    wgl_d = dt("wgl", [2, 128, 8 * 16], F32, kind="ExternalInput").ap()
    wga_d = dt("wga", [2, 17, 512], F32, kind="ExternalInput").ap()
    wogl_d = dt("wogl", [2, 4, 128, 2 * 1024], F32, kind="ExternalInput").ap()
    norms_d = dt("norms", [128, 12 * 8], F32, kind="ExternalInput").ap()
    gon_d = dt("gon", [128, 2 * 8], F32, kind="ExternalInput").ap()
    gqk_d = dt("gqk", [128, 4], F32, kind="ExternalInput").ap()
    cf_d = dt("cf", [128, 4 * 128], F32, kind="ExternalInput").ap()
    cb_d = dt("cb", [128, 9 * 128], F32, kind="ExternalInput").ap()
    y_d = dt("y", [SEQ, D], F32, kind="ExternalOutput").ap()
    if dbg:
        dq_d = dt("dq", [128, 2, T], BF16, kind="ExternalOutput").ap()
        dk_d = dt("dk", [128, 2, T], BF16, kind="ExternalOutput").ap()
        dv_d = dt("dv", [128, 17, 256], BF16, kind="ExternalOutput").ap()
        do_d = dt("do", [128, 2, T], BF16, kind="ExternalOutput").ap()

    with ExitStack() as st:
        S = Sched(nc, st)
        uniq = [0]

        def sb(name, shape, dtype, ctx=st):
            uniq[0] += 1
            return ctx.enter_context(nc.sbuf_tensor("%s_%d" % (name, uniq[0]), shape, dtype))
        hT = sb("hT", [128, 8, T], F32)
        yn = sb("yn", [128, 8, T], BF16)
        norms = sb("norms_s", [128, 96], F32)
        gon = sb("gon_s", [128, 16], F32)
        gqk = sb("gqk_s", [128, 4], F32)
        cf = sb("cf_s", [128, 4 * 128], F32)
        cb = sb("cb_s", [128, 9 * 128], BF16)
        epsT = sb("eps_s", [128, 1], F32)
        banks = [st.enter_context(nc.psum_tensor("bank%d" % i, [128, 512], F32)) for i in range(8)]
        bankb = [Buf() for _ in range(8)]
        ident = cf[:, 0:128]
        TIs = cb[:, 896:1024]
        SUs = cb[:, 1024:1152]
        mask01 = cf[:, 384:512]
        ones_mean = cb[:, 0:128]
        blk64 = cb[:, 128:256]
        ones256 = cb[:, 256:384]
        negtri = cb[:, 384:512]
        negones = cb[:, 512:640]
        identb = cb[:, 640:768]
        DM = cb[:, 768:896]
        constb = Buf()
        constb2 = Buf()
        h_b = [[Buf() for _ in range(5)] for _ in range(8)]
        yn_b = [Buf() for _ in range(5)]
        d_const = S.dma_sem()
        d_const2 = S.dma_sem()
        d_in = [S.dma_sem(), S.dma_sem()]
        d_out = [S.dma_sem(), S.dma_sem()]

        S.dma("sp", norms[:], norms_d, d_const, writes=[constb])
        S.dma("sp", gon[:], gon_d, d_const, writes=[constb])
        S.dma("sp", gqk[:], gqk_d, d_const, writes=[constb])
        S.dma("sp", cf[:], cf_d, d_const, writes=[constb])
        S.dma("pool", cb[:], cb_d, d_const2, writes=[constb2])
        S.op("dve", lambda: nc.vector.memset(epsT[:], EPS), writes=[constb])
        S.op("dve", lambda: nc.vector.tensor_scalar_mul(out=gqk[:, 0:1], in0=gqk[:, 0:1], scalar1=0.125),
             reads=[constb], writes=[constb])
        S.op("dve", lambda: nc.vector.tensor_scalar_mul(out=gqk[:, 2:3], in0=gqk[:, 2:3], scalar1=0.125),
             reads=[constb], writes=[constb])
        S.barrier()

        def tt_of_tok(i):
            return 4 if i == 16 else i // 4

        with ExitStack() as ph:
            xs = [sb("xs%d" % i, [128, D], F32, ph) for i in range(2)]
            xs_b = [Buf(), Buf()]
            for i, (c0, n) in enumerate(TOKT):
                s = i % 2
                src = x_d[c0:c0 + n, :] if i < 16 else meta_d
                S.dma("sp", xs[s][0:n, :], src, d_in[s], writes=[xs_b[s]])
                for half in range(2):
                    bk = (2 * i + half) % 4
                    pb = banks[bk]
                    for q in range(4):
                        kc = half * 4 + q
                        S.op("pe", lambda pb=pb, q=q, kc=kc, s=s, n=n: nc.tensor.transpose(
                            out=pb[:, q * 128:q * 128 + n], in_=xs[s][0:n, kc * 128:(kc + 1) * 128],
                            identity=ident[0:n, 0:n]),
                            reads=[xs_b[s], constb], writes=[bankb[bk]], inc=(q == 3))
                    src_ap = pb[:].rearrange("p (q c) -> p q c", q=4)[:, :, 0:n]
                    wr = [h_b[half * 4 + q][tt_of_tok(i)] for q in range(4)]
                    eng = "act" if half == 0 else "dve"
                    if eng == "act":
                        S.op("act", lambda src_ap=src_ap, half=half, c0=c0, n=n: nc.scalar.copy(
                            out=hT[:, half * 4:half * 4 + 4, c0:c0 + n], in_=src_ap),
                            reads=[bankb[bk]], writes=wr)
                    else:
                        S.op("dve", lambda src_ap=src_ap, half=half, c0=c0, n=n: nc.vector.tensor_copy(
                            out=hT[:, half * 4:half * 4 + 4, c0:c0 + n], in_=src_ap),
                            reads=[bankb[bk]], writes=wr)
            S.barrier()

        def rmsnorm(gidx, sq, sq_b, tmp, tmp_b, rstd, rstd_b):
            r = 0
            for ti, (c0, n) in enumerate(TT):
                bk = 6 + (ti % 2)
                for kc in range(8):
                    q = r % len(sq)
                    r += 1
                    S.op("act", lambda kc=kc, q=q, c0=c0, n=n: nc.scalar.activation(
                        out=sq[q][:, 0:n], in_=hT[:, kc, c0:c0 + n], func=AF.Square),
                        reads=[h_b[kc][ti]], writes=[sq_b[q]])
                    S.op("pe", lambda kc=kc, q=q, n=n, bk=bk: nc.tensor.matmul(
                        banks[bk][:, 0:n], ones_mean, sq[q][:, 0:n], start=(kc == 0), stop=(kc == 7)),
                        reads=[sq_b[q], constb2], writes=[bankb[bk]])
                pt = ti % len(tmp)
                pr = ti % len(rstd)
                S.op("act", lambda pt=pt, n=n, bk=bk: nc.scalar.activation(
                    out=tmp[pt][:, 0:n], in_=banks[bk][:, 0:n], func=AF.Ln, bias=epsT[:, 0:1], scale=1.0),
                    reads=[bankb[bk], constb], writes=[tmp_b[pt]])
                S.op("act", lambda pt=pt, pr=pr, n=n: nc.scalar.activation(
                    out=rstd[pr][:, 0:n], in_=tmp[pt][:, 0:n], func=AF.Exp, scale=-0.5),
                    reads=[tmp_b[pt]], writes=[rstd_b[pr]])
                for kc in range(8):
                    S.op("dve", lambda kc=kc, pr=pr, c0=c0, n=n: nc.vector.scalar_tensor_tensor(
                        out=yn[:, kc, c0:c0 + n], in0=hT[:, kc, c0:c0 + n],
                        scalar=norms[:, gidx * 8 + kc:gidx * 8 + kc + 1], in1=rstd[pr][:, 0:n],
                        op0=ALU.mult, op1=ALU.mult),
                        reads=[h_b[kc][ti], rstd_b[pr], constb], writes=[yn_b[ti]])

        def ffn(fidx, gidx):
            with ExitStack() as ph:
                n_sq = [sb("f_nsq%d" % i, [128, 512], BF16, ph) for i in range(4)]
                n_tmp = [sb("f_ntmp%d" % i, [128, 512], F32, ph) for i in range(2)]
                n_rs = [sb("f_nrs%d" % i, [128, 512], F32, ph) for i in range(2)]
                act = sb("f_act", [128, 11, T], BF16, ph)
                NGS = 5
                wgu = [sb("f_wgu%d" % i, [128, 8, 256], BF16, ph) for i in range(NGS)]
                wdn = [sb("f_wd%d" % i, [128, 11, 128], BF16, ph) for i in range(3)]
                sg = [sb("f_sg%d" % i, [128, 512], F32, ph) for i in range(2)]
                act_b = [[Buf() for _ in range(5)] for _ in range(11)]
                wgu_b = [Buf() for _ in range(NGS)]
                wdn_b = [Buf() for _ in range(3)]
                sg_b = [Buf(), Buf()]
                wgu_s = [S.dma_sem() for _ in range(NGS)]
                wdn_s = [S.dma_sem() for _ in range(3)]
                steps = []
                for grp in range(2):
                    for fci in range(11):
                        steps.append(("gu", grp, fci))
                    for dc in range(8):
                        steps.append(("d", grp, dc))
                kcount = {"gu": 0, "d": 0}
                slot_of = []
                for s_ in steps:
                    slot_of.append(kcount[s_[0]] % (NGS if s_[0] == "gu" else 3))
                    kcount[s_[0]] += 1
                loaded = [0]

                def load(m):
                    kind, grp, idx = steps[m]
                    s = slot_of[m]
                    if kind == "gu":
                        fc = grp * 11 + idx
                        S.dma("pool", wgu[s][:].rearrange("p a b -> p (a b)"), wgu_d[fidx, fc],
                              wgu_s[s], writes=[wgu_b[s]])
                    else:
                        S.dma("pool", wdn[s][:].rearrange("p a b -> p (a b)"), wd_d[fidx, grp, idx],
                              wdn_s[s], writes=[wdn_b[s]])

                for m0 in range(NGS):
                    load(m0)
                loaded[0] = NGS
                rmsnorm(gidx, n_sq, [Buf() for _ in n_sq], n_tmp, [Buf() for _ in n_tmp], n_rs, [Buf() for _ in n_rs])
                gstate = [0]

                def gu_tile(m, ti):
                    kind, grp, fci = steps[m]
                    s = slot_of[m]
                    c0, n = TT[ti]
                    p = gstate[0] % 2
                    gstate[0] += 1
                    bg, bu = p, 2 + p
                    for kc in range(8):
                        S.op("pe", lambda kc=kc: nc.tensor.matmul(
                            banks[bg][:, 0:n], wgu[s][:, kc, 0:128], yn[:, kc, c0:c0 + n],
                            start=(kc == 0), stop=(kc == 7)),
                            reads=[wgu_b[s], yn_b[ti]], writes=[bankb[bg]], inc=(kc == 7))
                    for kc in range(8):
                        S.op("pe", lambda kc=kc: nc.tensor.matmul(
                            banks[bu][:, 0:n], wgu[s][:, kc, 128:256], yn[:, kc, c0:c0 + n],
                            start=(kc == 0), stop=(kc == 7)),
                            reads=[wgu_b[s], yn_b[ti]], writes=[bankb[bu]], inc=(kc == 7))
                    S.op("act", lambda: nc.scalar.activation(
                        out=sg[p][:, 0:n], in_=banks[bg][:, 0:n], func=AF.Silu),
                        reads=[bankb[bg]], writes=[sg_b[p]])
                    S.op("dve", lambda: nc.vector.tensor_tensor(
                        out=act[:, fci, c0:c0 + n], in0=sg[p][:, 0:n], in1=banks[bu][:, 0:n], op=ALU.mult),
                        reads=[sg_b[p], bankb[bu]], writes=[act_b[fci][ti]])

                for ti in range(len(TT)):
                    for m in range(3):
                        gu_tile(m, ti)
                gcount = 0
                for m, (kind, grp, idx) in enumerate(steps):
                    if m < 3:
                        continue
                    while loaded[0] < len(steps) and loaded[0] <= m + 2:
                        load(loaded[0])
                        loaded[0] += 1
                    s = slot_of[m]
                    if kind == "gu":
                        for ti in range(len(TT)):
                            gu_tile(m, ti)
                    else:
                        dc = idx
                        for ti, (c0, n) in enumerate(TT):
                            p = gcount % 2
                            gcount += 1
                            bo = 4 + p
                            for fci in range(11):
                                S.op("pe", lambda fci=fci, s=s, c0=c0, n=n, bo=bo: nc.tensor.matmul(
                                    banks[bo][:, 0:n], wdn[s][:, fci, :], act[:, fci, c0:c0 + n],
                                    start=(fci == 0), stop=(fci == 10)),
                                    reads=[wdn_b[s], act_b[fci][ti]], writes=[bankb[bo]], inc=(fci == 10))
                            S.op("dve", lambda dc=dc, c0=c0, n=n, bo=bo: nc.vector.scalar_tensor_tensor(
                                out=hT[:, dc, c0:c0 + n], in0=banks[bo][:, 0:n], scalar=0.5,
                                in1=hT[:, dc, c0:c0 + n], op0=ALU.mult, op1=ALU.add),
                                reads=[bankb[bo], h_b[dc][ti]], writes=[h_b[dc][ti]])
                S.barrier()

        def rstd_from(bk, np_, n, tmp_t, tmp_b, out_t, out_b):
            S.op("act", lambda: nc.scalar.activation(
                out=tmp_t[0:np_, 0:n], in_=banks[bk][0:np_, 0:n], func=AF.Ln, bias=epsT[0:np_, 0:1], scale=1.0),
                reads=[bankb[bk], constb], writes=[tmp_b])
            S.op("act", lambda: nc.scalar.activation(
                out=out_t[0:np_, 0:n], in_=tmp_t[0:np_, 0:n], func=AF.Exp, scale=-0.5),
                reads=[tmp_b], writes=[out_b])

        def sb_mixer(j, gidx):
            with ExitStack() as ph:
                wq = [sb("s_wq%d" % i, [128, 8, 768], BF16, ph) for i in range(2)]
                wo = [sb("s_wo%d" % i, [128, 2, 1024], BF16, ph) for i in range(2)]
                qT = sb("s_qT", [128, 2, T], BF16, ph)
                kT = sb("s_kT", [128, 2, T], BF16, ph)
                vg = sb("s_v", [128, 17, 256], BF16, ph)
                oT = sb("s_oT", [128, 2, T], BF16, ph)
                sq = [sb("s_sq%d" % i, [128, 512], BF16, ph) for i in range(2)]
                tmp = [sb("s_tmp%d" % i, [128, 512], F32, ph) for i in range(2)]
                rs = [sb("s_rs%d" % i, [128, 512], F32, ph) for i in range(2)]
                et = [sb("s_e%d" % i, [128, 512], F32, ph) for i in range(4)]
                spt = [sb("s_sp%d" % i, [128, 512], BF16, ph) for i in range(4)]
                wt = [sb("s_w%d" % i, [128, 512], BF16, ph) for i in range(4)]
                A32 = [sb("s_A32_%d" % i, [128, 512], F32, ph) for i in range(2)]
                Abf = [sb("s_Abf_%d" % i, [128, 512], BF16, ph) for i in range(2)]
                wq_b = [Buf(), Buf()]
                wo_b = [Buf(), Buf()]
                wq_s = [S.dma_sem(), S.dma_sem()]
                wo_s = [S.dma_sem(), S.dma_sem()]
                qT_b = [[Buf() for _ in range(5)] for _ in range(2)]
                kT_b = [[Buf() for _ in range(5)] for _ in range(2)]
                v_b = [Buf() for _ in range(17)]
                oT_b = [[Buf() for _ in range(5)] for _ in range(2)]
                sq_b = [Buf(), Buf()]
                tmp_b = [Buf(), Buf()]
                rs_b = [Buf(), Buf()]
                et_b = [Buf() for _ in range(4)]
                sp_b = [Buf() for _ in range(4)]
                w_b = [Buf() for _ in range(4)]
                A32_b = [Buf(), Buf()]
                Abf_b = [Buf(), Buf()]
                rmsnorm(gidx, sq, sq_b, tmp, tmp_b, rs, rs_b)

                def loadw(hg):
                    s = hg % 2
                    S.dma("pool", wq[s][:].rearrange("p a b -> p (a b)"), wqkv_d[j, hg], wq_s[s], writes=[wq_b[s]])
                    S.dma("pool", wo[s][:].rearrange("p a b -> p (a b)"), wosb_d[j, hg], wo_s[s], writes=[wo_b[s]])

                loadw(0)
                pcount = 0
                for hg in range(4):
                    if hg + 1 < 4:
                        loadw(hg + 1)
                    ws = hg % 2
                    ptiles = []
                    for which, dst, dst_b, gcol in ((0, qT, qT_b, 2 * j), (1, kT, kT_b, 2 * j + 1)):
                        for c in range(2):
                            for ti, (c0, n) in enumerate(TT):
                                ptiles.append((which, dst, dst_b, gcol, c, ti, c0, n))

                    def p_stage1(t):
                        which, dst, dst_b, gcol, c, ti, c0, n = ptiles[t]
                        bk = 2 * (t % 4)
                        col = which * 256 + c * 128
                        for kc in range(8):
                            S.op("pe", lambda kc=kc: nc.tensor.matmul(
                                banks[bk][:, 0:n], wq[ws][:, kc, col:col + 128], yn[:, kc, c0:c0 + n],
                                start=(kc == 0), stop=(kc == 7)),
                                reads=[wq_b[ws], yn_b[ti]], writes=[bankb[bk]], inc=(kc == 7))

                    def p_stage2(t):
                        which, dst, dst_b, gcol, c, ti, c0, n = ptiles[t]
                        bk = 2 * (t % 4)
                        bk2 = bk + 1
                        p = t % 2
                        S.op("act", lambda: nc.scalar.activation(
                            out=sq[p][:, 0:n], in_=banks[bk][:, 0:n], func=AF.Square),
                            reads=[bankb[bk]], writes=[sq_b[p]])
                        S.op("pe", lambda: nc.tensor.matmul(
                            banks[bk2][:, 0:n], blk64, sq[p][:, 0:n], start=True, stop=True),
                            reads=[sq_b[p], constb], writes=[bankb[bk2]])
                        rstd_from(bk2, 128, n, tmp[p], tmp_b[p], rs[p], rs_b[p])
                        S.op("dve", lambda: nc.vector.scalar_tensor_tensor(
                            out=dst[:, c, c0:c0 + n], in0=banks[bk][:, 0:n], scalar=gqk[:, gcol:gcol + 1],
                            in1=rs[p][:, 0:n], op0=ALU.mult, op1=ALU.mult),
                            reads=[bankb[bk], rs_b[p], constb], writes=[dst_b[c][ti]])

                    p_stage1(0)
                    for t in range(len(ptiles)):
                        if t + 1 < len(ptiles):
                            p_stage1(t + 1)
                        p_stage2(t)
                    for i, (c0, n) in enumerate(TOKT):
                        bk = pcount % 8
                        pcount += 1
                        for kc in range(8):
                            S.op("pe", lambda kc=kc, c0=c0, n=n, bk=bk: nc.tensor.matmul(
                                banks[bk][0:n, 0:256], yn[:, kc, c0:c0 + n], wq[ws][:, kc, 512:768],
                                start=(kc == 0), stop=(kc == 7)),
                                reads=[wq_b[ws], yn_b[tt_of_tok(i)]], writes=[bankb[bk]], inc=(kc == 7))
                        if i % 2 == 0:
                            S.op("act", lambda i=i, n=n, bk=bk: nc.scalar.copy(
                                out=vg[0:n, i, :], in_=banks[bk][0:n, 0:256]), reads=[bankb[bk]], writes=[v_b[i]])
                        else:
                            S.op("dve", lambda i=i, n=n, bk=bk: nc.vector.tensor_copy(
                                out=vg[0:n, i, :], in_=banks[bk][0:n, 0:256]), reads=[bankb[bk]], writes=[v_b[i]])
                    batches = []
                    for c in range(2):
                        for q4 in range(4):
                            for Sb in range(4 * q4 + 3, -1, -1):
                                batches.append((c, q4, Sb))
                            batches.append((c, q4, 16))
                        batches.append((c, 4, 16))
                    ocount = [0]
                    o_bank_of = {}

                    def geom(b):
                        c, q4, Sb = b
                        if q4 == 4:
                            return dict(c=c, q4=4, Sb=16, nk=16, k0=2048, qc0=2048, lc0=0, n=16, diag=True,
                                        first=True, last=True, useA=False)
                        if Sb == 16:
                            return dict(c=c, q4=q4, Sb=16, nk=16, k0=2048, qc0=512 * q4, lc0=0, n=512, diag=False,
                                        first=False, last=True, useA=True)
                        lc0 = max(0, Sb - 4 * q4) * 128
                        return dict(c=c, q4=q4, Sb=Sb, nk=128, k0=Sb * 128, qc0=512 * q4 + lc0, lc0=lc0, n=512 - lc0,
                                    diag=(Sb >= 4 * q4), first=(Sb == 4 * q4 + 3), last=False,
                                    useA=(Sb != 4 * q4 + 3))

                    def zmm(bi, g, hh):
                        pbase = hh * 64
                        zb = (bi % 3) * 2 + hh
                        nk, n = g["nk"], g["n"]
                        ti_q = g["q4"]
                        ti_k = tt_of_tok(g["Sb"])
                        kap = kT[pbase:pbase + 64, g["c"], g["k0"]:g["k0"] + nk]
                        rd = [kT_b[g["c"]][ti_k], qT_b[g["c"]][ti_q]]
                        S.op("pe", lambda: nc.tensor.matmul(
                            banks[zb][0:nk, 0:n], kap, qT[pbase:pbase + 64, g["c"], g["qc0"]:g["qc0"] + n],
                            start=True, stop=False), reads=rd, writes=[bankb[zb]], inc=(not g["diag"]))
                        if g["diag"]:
                            nd = min(128, n)
                            S.op("pe", lambda: nc.tensor.matmul(
                                banks[zb][0:nk, 0:nd], identb[0:nk, 0:nk], DM[0:nk, 0:nd], start=False, stop=False),
                                reads=[constb], writes=[bankb[zb]])

                    def el(bi, g, hh):
                        zb = (bi % 3) * 2 + hh
                        ei = (2 * bi + hh) % 4
                        nk, n = g["nk"], g["n"]
                        S.op("act", lambda: nc.scalar.activation(
                            out=et[ei][0:nk, 0:n], in_=banks[zb][0:nk, 0:n], func=AF.Exp),
                            reads=[bankb[zb]], writes=[et_b[ei]])
                        S.op("act", lambda: nc.scalar.activation(
                            out=spt[ei][0:nk, 0:n], in_=et[ei][0:nk, 0:n], func=AF.Ln, bias=1.0, scale=1.0),
                            reads=[et_b[ei]], writes=[sp_b[ei]])

                    def ta(bi, g, hh):
                        zb = (bi % 3) * 2 + hh
                        ei = (2 * bi + hh) % 4
                        nk, n, lc0 = g["nk"], g["n"], g["lc0"]
                        S.op("pe", lambda: nc.tensor.matmul(
                            banks[zb][0:nk, 0:n], negtri[0:nk, 0:nk], spt[ei][0:nk, 0:n], start=False,
                            stop=(not g["useA"])), reads=[sp_b[ei], constb, et_b[ei]], writes=[bankb[zb]],
                            inc=(not g["useA"]))
                        if g["useA"]:
                            S.op("pe", lambda: nc.tensor.matmul(
                                banks[zb][0:nk, 0:n], negones[:, 0:nk], Abf[hh][:, lc0:lc0 + n], start=False, stop=True),
                                reads=[Abf_b[hh], constb], writes=[bankb[zb]])

                    def aupd(bi, g, hh):
                        if g["last"]:
                            return
                        zb = (bi % 3) * 2 + hh
                        ei = (2 * bi + hh) % 4
                        n, lc0 = g["n"], g["lc0"]
                        if g["first"]:
                            S.op("pool", lambda: nc.gpsimd.memset(A32[hh][:], 0.0), writes=[A32_b[hh]])
                        S.op("pool", lambda: nc.gpsimd.tensor_tensor(
                            out=A32[hh][:, lc0:lc0 + n], in0=A32[hh][:, lc0:lc0 + n], in1=spt[ei][:, 0:n], op=ALU.add),
                            reads=[sp_b[ei], A32_b[hh]], writes=[A32_b[hh]])
                        if g["first"]:
                            S.op("dve", lambda: nc.vector.tensor_copy(out=Abf[hh][:], in_=A32[hh][:]),
                                 reads=[A32_b[hh]], writes=[Abf_b[hh]])
                        else:
                            S.op("dve", lambda: nc.vector.tensor_copy(out=Abf[hh][:, lc0:lc0 + n],
                                                                      in_=A32[hh][:, lc0:lc0 + n]),
                                 reads=[A32_b[hh]], writes=[Abf_b[hh]])

                    def ea(bi, g, hh):
                        zb = (bi % 3) * 2 + hh
                        ei = (2 * bi + hh) % 4
                        nk, n = g["nk"], g["n"]
                        S.op("act", lambda: nc.scalar.activation(
                            out=wt[ei][0:nk, 0:n], in_=banks[zb][0:nk, 0:n], func=AF.Exp),
                            reads=[bankb[zb]], writes=[w_b[ei]])

                    def pv(bi, g, hh):
                        zb = (bi % 3) * 2 + hh
                        ei = (2 * bi + hh) % 4
                        pbase = hh * 64
                        nk, n, lc0 = g["nk"], g["n"], g["lc0"]
                        key = (g["c"], g["q4"])
                        if key not in o_bank_of:
                            o_bank_of[key] = 6 + (ocount[0] % 2)
                            ocount[0] += 1
                        ob = o_bank_of[key]
                        vi = g["Sb"]
                        vap = vg[0:nk, vi, (2 * g["c"] + hh) * 64:(2 * g["c"] + hh) * 64 + 64]
                        tp = (0, pbase)
                        rd = [v_b[vi], w_b[ei]]
                        S.op("pe", lambda: nc.tensor.matmul(
                            banks[ob][pbase:pbase + 64, lc0:lc0 + n], vap, wt[ei][0:nk, 0:n], start=g["first"],
                            stop=g["last"], tile_position=tp), reads=rd, writes=[bankb[ob]])

                    def oevac(g):
                        key = (g["c"], g["q4"])
                        ob = o_bank_of[key]
                        n = 16 if g["q4"] == 4 else 512
                        c0 = 2048 if g["q4"] == 4 else 512 * g["q4"]
                        S.op("dve", lambda: nc.vector.tensor_copy(out=oT[:, g["c"], c0:c0 + n], in_=banks[ob][:, 0:n]),
                             reads=[bankb[ob]], writes=[oT_b[g["c"]][g["q4"]]])

                    geoms = [geom(b) for b in batches]
                    nb = len(geoms)
                    for b0_ in range(min(2, nb)):
                        for hh in range(2):
                            zmm(b0_, geoms[b0_], hh)
                    for k in range(2 * nb):
                        bi, hh = divmod(k, 2)
                        g = geoms[bi]
                        el(bi, g, hh)
                        if k >= 1:
                            pb_, ph_ = divmod(k - 1, 2)
                            ea(pb_, geoms[pb_], ph_)
                        if hh == 1 and bi + 2 < nb:
                            for h2 in range(2):
                                zmm(bi + 2, geoms[bi + 2], h2)
                        ta(bi, g, hh)
                        aupd(bi, g, hh)
                        if k >= 1:
                            pv(pb_, geoms[pb_], ph_)
                            if ph_ == 1 and geoms[pb_]["last"]:
                                oevac(geoms[pb_])
                    ea(nb - 1, geoms[nb - 1], 1)
                    pv(nb - 1, geoms[nb - 1], 1)
                    oevac(geoms[nb - 1])
                    if dbg and hg == 0:
                        dsem = S.dma_sem()
                        S.dma("sp", dq_d, qT[:], dsem, reads=[b for r in qT_b for b in r])
                        S.dma("sp", dk_d, kT[:], dsem, reads=[b for r in kT_b for b in r])
                        S.dma("sp", dv_d, vg[:], dsem, reads=v_b)
                        S.dma("sp", do_d, oT[:], dsem, reads=[b for r in oT_b for b in r])
                    for dc in range(8):
                        for ti, (c0, n) in enumerate(TT):
                            bk = pcount % 8
                            pcount += 1
                            for c in range(2):
                                S.op("pe", lambda c=c, dc=dc, c0=c0, n=n, bk=bk: nc.tensor.matmul(
                                    banks[bk][:, 0:n], wo[ws][:, c, dc * 128:(dc + 1) * 128], oT[:, c, c0:c0 + n],
                                    start=(c == 0), stop=(c == 1)),
                                    reads=[wo_b[ws], oT_b[c][ti]], writes=[bankb[bk]], inc=(c == 1))
                            S.op("dve", lambda dc=dc, c0=c0, n=n, bk=bk: nc.vector.tensor_tensor(
                                out=hT[:, dc, c0:c0 + n], in0=hT[:, dc, c0:c0 + n], in1=banks[bk][:, 0:n], op=ALU.add),
                                reads=[bankb[bk], h_b[dc][ti]], writes=[h_b[dc][ti]])
                S.barrier()

        def gla_mixer(j, gidx):
            with ExitStack() as ph:
                wqkv_t = sb("g_wqkv", [128, 8, 512], BF16, ph)
                wr_t = sb("g_wr", [128, 8, 256], BF16, ph)
                wo_t = sb("g_wo", [128, 2, 1024], BF16, ph)
                wgl = sb("g_wgl", [128, 8, 16], BF16, ph)
                wga = sb("g_wga", [32, 512], BF16, ph)
                glT = sb("g_glT", [32, T], BF16, ph)
                qd = sb("g_qd", [128, T], BF16, ph)
                kd = sb("g_kd", [128, T], BF16, ph)
                kst = sb("g_kst", [128, 17, 128], BF16, ph)
                vt = sb("g_vt", [128, 17, 256], BF16, ph)
                Eb = sb("g_Eb", [128, T], F32, ph)
                Enb = sb("g_Enb", [128, T], F32, ph)
                Eaft = sb("g_Eaft", [128, 17, 128], F32, ph)
                et = [sb("g_e%d" % i, [128, 128], F32, ph) for i in range(2)]
                spt = [sb("g_sp%d" % i, [128, 128], F32, ph) for i in range(2)]
                shi = [sb("g_shi%d" % i, [128, 128], BF16, ph) for i in range(2)]
                slo = [sb("g_slo%d" % i, [128, 128], BF16, ph) for i in range(2)]
                shi_b = [Buf(), Buf()]
                slo_b = [Buf(), Buf()]
                oTt = sb("g_oT", [128, 2, T], F32, ph)
                rstd = Enb
                tmp = [sb("g_tmp", [128, 512], F32, ph)] * 2
                sq = [sb("g_sq", [128, 2, 512], BF16, ph)] * 2
                sr = [sb("g_sr", [128, 2, 512], BF16, ph)] * 2
                t1 = [sb("g_t1", [128, 2, 512], F32, ph)] * 2
                og = [sb("g_og%d" % i, [128, 2, 512], BF16, ph) for i in range(2)]
                S32 = sb("g_S32", [128, 256], F32, ph)
                Sbf2 = [sb("g_Sbf%d" % i, [128, 256], BF16, ph) for i in range(2)]
                attm = [sb("g_att%d" % i, [128, 128], BF16, ph) for i in range(2)]
                wqkv_b, wr_b, wo_b = Buf(), Buf(), Buf()
                wqkv_s, wr_s, wo_s = S.dma_sem(), S.dma_sem(), S.dma_sem()
                wg_s = S.dma_sem()
                wg_b = Buf()
                glT_b = [Buf() for _ in range(5)]
                qd_b = [Buf() for _ in range(5)]
                kd_b = [Buf() for _ in range(5)]
                kst_b = [Buf() for _ in range(17)]
                vt_b = [Buf() for _ in range(17)]
                Eb_b = [Buf() for _ in range(17)]
                Enb_b = [Buf() for _ in range(17)]
                Eaft_b = [Buf() for _ in range(17)]
                et_b = [Buf(), Buf()]
                sp_b = [Buf(), Buf()]
                oT_b = [Buf() for _ in range(17)]
                rstd_b = [Buf() for _ in range(5)]
                tmp_b = [Buf()] * 2
                sq_b = [Buf()] * 2
                sr_b = [Buf()] * 2
                t1_b = [Buf()] * 2
                og_b = [Buf(), Buf()]
                S32_b = Buf()
                Sbf2_b = [Buf(), Buf()]
                att_b = [Buf(), Buf()]
                rmsnorm(gidx, [sq[0][:, 0, :], sq[0][:, 1, :]], [Buf(), Buf()], [tmp[0]], [tmp_b[0]],
                        [t1[0][:, 0, :], t1[0][:, 1, :]], [Buf(), Buf()])
                S.barrier()

                def load_qkv(hd):
                    S.dma("pool", wqkv_t[:].rearrange("p a b -> p (a b)"), win_d[j, hd], wqkv_s, writes=[wqkv_b])

                def load_ro(hd):
                    S.dma("pool", wr_t[:].rearrange("p a b -> p (a b)"), winr_d[j, hd], wr_s, writes=[wr_b])
                    S.dma("pool", wo_t[:].rearrange("p a b -> p (a b)"), wogl_d[j, hd], wo_s, writes=[wo_b])

                S.dma("pool", wgl[:].rearrange("p a b -> p (a b)"), wgl_d[j], wg_s, writes=[wg_b])
                S.op("pool", lambda: nc.gpsimd.memset(wga[:], 0.0), writes=[wg_b])
                S.dma("pool", wga[0:17, :], wga_d[j], wg_s, writes=[wg_b])
                load_qkv(0)
                load_ro(0)
                S.op("dve", lambda: nc.vector.memset(glT[:], 1.0), writes=glT_b)
                pcount = 0
                for ti, (c0, n) in enumerate(TT):
                    p = pcount % 2
                    pcount += 1
                    bk = 6 + p
                    for kc in range(8):
                        S.op("pe", lambda kc=kc, c0=c0, n=n, bk=bk: nc.tensor.matmul(
                            banks[bk][0:16, 0:n], wgl[:, kc, :], yn[:, kc, c0:c0 + n], start=(kc == 0), stop=(kc == 7)),
                            reads=[wg_b, yn_b[ti]], writes=[bankb[bk]], inc=(kc == 7))
                    S.op("dve", lambda c0=c0, n=n, bk=bk: nc.vector.tensor_copy(
                        out=glT[0:16, c0:c0 + n], in_=banks[bk][0:16, 0:n]),
                        reads=[bankb[bk]], writes=[glT_b[ti]])

                for hd in range(4):
                    for gi, (c0g, ng) in enumerate(TT):
                        tiles = [16] if gi == 4 else [4 * gi + u for u in range(4)]
                        L = len(tiles)
                        n = TOKT[tiles[0]][1]
                        W = L * 128 if n == 128 else n
                        p = gi % 2
                        b0, b1, b2 = 0 + p, 2 + p, 4 + p
                        spv = t1[0][:, p, :]
                        hi = sq[0][:, 0, :] if p == 0 else og[0][:, 0, :]
                        lo = sq[0][:, 1, :] if p == 0 else og[0][:, 1, :]
                        hl_b = sq_b[0] if p == 0 else og_b[0]
                        for u, ti_ in enumerate(tiles):
                            c0 = TOKT[ti_][0]
                            S.op("pe", lambda c0=c0, n=n, b0=b0, u=u: nc.tensor.matmul(
                                banks[b0][0:n, u * 128:(u + 1) * 128], glT[0:32, c0:c0 + n],
                                wga[0:32, hd * 128:(hd + 1) * 128], start=True, stop=True),
                                reads=[glT_b[gi], wg_b], writes=[bankb[b0]], inc=(u == L - 1))
                        Wg = L * 128
                        S.op("act", lambda n=n, b0=b0, Wg=Wg: nc.scalar.activation(
                            out=tmp[0][0:n, 0:Wg], in_=banks[b0][0:n, 0:Wg], func=AF.Exp, scale=-1.0),
                            reads=[bankb[b0]], writes=[tmp_b[0]])
                        S.op("act", lambda n=n, Wg=Wg, spv=spv: nc.scalar.activation(
                            out=spv[0:n, 0:Wg], in_=tmp[0][0:n, 0:Wg], func=AF.Ln, bias=1.0, scale=1.0),
                            reads=[tmp_b[0]], writes=[t1_b[0]])
                        S.op("dve", lambda n=n, Wg=Wg, spv=spv, hi=hi: nc.vector.tensor_copy(
                            out=hi[0:n, 0:Wg], in_=spv[0:n, 0:Wg]), reads=[t1_b[0]], writes=[hl_b])
                        S.op("dve", lambda n=n, Wg=Wg, spv=spv, hi=hi, lo=lo: nc.vector.tensor_tensor(
                            out=lo[0:n, 0:Wg], in0=spv[0:n, 0:Wg], in1=hi[0:n, 0:Wg], op=ALU.subtract),
                            reads=[t1_b[0], hl_b], writes=[hl_b])
                        for u in range(L):
                            for part, src in enumerate((hi, lo)):
                                S.op("pe", lambda n=n, b1=b1, u=u, src=src, part=part: nc.tensor.matmul(
                                    banks[b1][:, u * 128:u * 128 + n], src[0:n, u * 128:(u + 1) * 128], TIs[0:n, 0:n],
                                    start=(part == 0), stop=(part == 1)),
                                    reads=[hl_b, constb], writes=[bankb[b1]], inc=(u == L - 1 and part == 1))
                        for u in range(L):
                            for part, src in enumerate((hi, lo)):
                                S.op("pe", lambda n=n, b2=b2, u=u, src=src, part=part: nc.tensor.matmul(
                                    banks[b2][0:n, u * 128:(u + 1) * 128], SUs[0:n, 0:n], src[0:n, u * 128:(u + 1) * 128],
                                    start=(part == 0), stop=(part == 1)),
                                    reads=[hl_b, constb], writes=[bankb[b2]], inc=(u == L - 1 and part == 1))
                        S.op("act", lambda c0g=c0g, ng=ng, b1=b1: nc.scalar.activation(
                            out=Eb[:, c0g:c0g + ng], in_=banks[b1][:, 0:ng], func=AF.Exp),
                            reads=[bankb[b1]], writes=[Eb_b[u_] for u_ in tiles])
                        S.op("act", lambda c0g=c0g, ng=ng, b1=b1: nc.scalar.activation(
                            out=Enb[:, c0g:c0g + ng], in_=banks[b1][:, 0:ng], func=AF.Exp, scale=-1.0),
                            reads=[bankb[b1]], writes=[Enb_b[u_] for u_ in tiles] + [rstd_b[gi]])
                        t0_ = tiles[0]
                        S.op("act", lambda n=n, b2=b2, Wg=Wg, t0_=t0_, L=L: nc.scalar.activation(
                            out=Eaft[0:n, t0_:t0_ + L, :],
                            in_=banks[b2][0:n, 0:Wg].rearrange("p (l k) -> p l k", l=L), func=AF.Exp),
                            reads=[bankb[b2]], writes=[Eaft_b[u_] for u_ in tiles])
                    if gla_steps < 2:
                        continue
                    for ti, (c0, n) in enumerate(TT):
                        tiles_in = [16] if ti == 4 else [4 * ti + u for u in range(4)]
                        r4 = pcount % 4
                        pcount += 1
                        bq, bkk = 2 * r4, 2 * r4 + 1
                        for kc in range(8):
                            S.op("pe", lambda kc=kc, c0=c0, n=n, bq=bq: nc.tensor.matmul(
                                banks[bq][:, 0:n], wqkv_t[:, kc, 0:128], yn[:, kc, c0:c0 + n],
                                start=(kc == 0), stop=(kc == 7)),
                                reads=[wqkv_b, yn_b[ti]], writes=[bankb[bq]], inc=(kc == 7))
                        for kc in range(8):
                            S.op("pe", lambda kc=kc, c0=c0, n=n, bkk=bkk: nc.tensor.matmul(
                                banks[bkk][:, 0:n], wqkv_t[:, kc, 128:256], yn[:, kc, c0:c0 + n],
                                start=(kc == 0), stop=(kc == 7)),
                                reads=[wqkv_b, yn_b[ti]], writes=[bankb[bkk]], inc=(kc == 7))
                        S.op("dve", lambda c0=c0, n=n, bq=bq: nc.vector.scalar_tensor_tensor(
                            out=qd[:, c0:c0 + n], in0=banks[bq][:, 0:n], scalar=128.0 ** -0.5, in1=Eb[:, c0:c0 + n],
                            op0=ALU.mult, op1=ALU.mult),
                            reads=[bankb[bq]] + [Eb_b[u] for u in tiles_in], writes=[qd_b[ti]])
                        S.op("dve", lambda c0=c0, n=n, bkk=bkk: nc.vector.tensor_tensor(
                            out=kd[:, c0:c0 + n], in0=Enb[:, c0:c0 + n], in1=banks[bkk][:, 0:n], op=ALU.mult),
                            reads=[bankb[bkk]] + [Enb_b[u] for u in tiles_in], writes=[kd_b[ti]])
                    for i, (c0, n) in enumerate(TOKT if gla_steps >= 2.5 else []):
                        bk = pcount % 8
                        pcount += 1
                        for kc in range(8):
                            S.op("pe", lambda kc=kc, c0=c0, n=n, bk=bk: nc.tensor.matmul(
                                banks[bk][0:n, 0:384], yn[:, kc, c0:c0 + n], wqkv_t[:, kc, 128:512],
                                start=(kc == 0), stop=(kc == 7)),
                                reads=[wqkv_b, yn_b[tt_of_tok(i)]], writes=[bankb[bk]], inc=(kc == 7))
                        S.op("dve", lambda i=i, n=n, bk=bk: nc.vector.tensor_tensor(
                            out=kst[0:n, i, :], in0=Eaft[0:n, i, :], in1=banks[bk][0:n, 0:128], op=ALU.mult),
                            reads=[bankb[bk], Eaft_b[i]], writes=[kst_b[i]])
                        S.op("act", lambda i=i, n=n, bk=bk: nc.scalar.copy(out=vt[0:n, i, :], in_=banks[bk][0:n, 128:384]),
                             reads=[bankb[bk], kst_b[i]], writes=[vt_b[i]])
                    if gla_steps < 3:
                        continue
                    if hd + 1 < 4:
                        load_qkv(hd + 1)
                    S.op("dve", lambda: nc.vector.memset(S32[:], 0.0), writes=[S32_b])
                    order = [16] + list(range(16))

                    def c_att(oi):
                        i = order[oi]
                        c0, n = TOKT[i]
                        ti = tt_of_tok(i)
                        p = oi % 2
                        ba = 0 + p
                        S.op("pe", lambda: nc.tensor.matmul(
                            banks[ba][0:n, 0:n], kd[:, c0:c0 + n], qd[:, c0:c0 + n], start=True, stop=True),
                            reads=[kd_b[ti], qd_b[ti]], writes=[bankb[ba]])
                        S.op("dve", lambda: nc.vector.tensor_tensor(
                            out=attm[p][0:n, 0:n], in0=mask01[0:n, 0:n], in1=banks[ba][0:n, 0:n], op=ALU.mult),
                            reads=[bankb[ba], constb], writes=[att_b[p]])

                    def c_kv(oi):
                        i = order[oi]
                        c0, n = TOKT[i]
                        bs = 4 + (oi % 4)
                        S.op("pe", lambda: nc.tensor.matmul(
                            banks[bs][:, 0:256], kst[0:n, i, :], vt[0:n, i, :], start=True, stop=True),
                            reads=[kst_b[i], vt_b[i]], writes=[bankb[bs]])

                    c_att(0)
                    c_kv(0)
                    for oi, i in enumerate(order):
                        c0, n = TOKT[i]
                        ti = tt_of_tok(i)
                        p = oi % 2
                        bo, bs = 2 + p, 4 + (oi % 4)
                        Sprev, Sprev_b = Sbf2[(oi + 1) % 2], Sbf2_b[(oi + 1) % 2]
                        Snew, Snew_b = Sbf2[oi % 2], Sbf2_b[oi % 2]
                        if oi + 1 < len(order):
                            c_att(oi + 1)
                            c_kv(oi + 1)
                        for c in range(2):
                            S.op("pe", lambda c=c, i=i, p=p, n=n, bo=bo: nc.tensor.matmul(
                                banks[bo][:, c * 128:c * 128 + n], vt[0:n, i, c * 128:(c + 1) * 128], attm[p][0:n, 0:n],
                                start=True, stop=(oi == 0)),
                                reads=[vt_b[i], att_b[p]], writes=[bankb[bo]], inc=(oi == 0 and c == 1))
                            if oi > 0:
                                S.op("pe", lambda c=c, c0=c0, n=n, bo=bo: nc.tensor.matmul(
                                    banks[bo][:, c * 128:c * 128 + n], Sprev[:, c * 128:(c + 1) * 128], qd[:, c0:c0 + n],
                                    start=False, stop=True),
                                    reads=[Sprev_b, qd_b[ti]], writes=[bankb[bo]], inc=(c == 1))
                        src_ap = banks[bo][:, 0:256].rearrange("p (c q) -> p c q", c=2)[:, :, 0:n]
                        S.op("act", lambda src_ap=src_ap, c0=c0, n=n: nc.scalar.copy(
                            out=oTt[:, :, c0:c0 + n], in_=src_ap),
                            reads=[bankb[bo]], writes=[oT_b[i]])
                        S.op("dve", lambda c0=c0, n=n, bs=bs: nc.vector.scalar_tensor_tensor(
                            out=Snew[:], in0=S32[:], scalar=Eb[:, c0 + n - 1:c0 + n], in1=banks[bs][:, 0:256],
                            op0=ALU.mult, op1=ALU.add),
                            reads=[S32_b, Eb_b[i], bankb[bs]], writes=[Snew_b])
                        S.op("dve", lambda c0=c0, n=n, bs=bs: nc.vector.scalar_tensor_tensor(
                            out=S32[:], in0=S32[:], scalar=Eb[:, c0 + n - 1:c0 + n], in1=banks[bs][:, 0:256],
                            op0=ALU.mult, op1=ALU.add),
                            reads=[S32_b, Eb_b[i], bankb[bs]], writes=[S32_b])
                    if gla_steps < 4:
                        continue
                    for ti, (c0, n) in enumerate(TT):
                        tiles_in = [16] if ti == 4 else [4 * ti + u for u in range(4)]
                        p = ti % 2
                        bk = 6 + p
                        S.op("act", lambda p=p, c0=c0, n=n: nc.scalar.activation(
                            out=sq[p][:, :, 0:n], in_=oTt[:, :, c0:c0 + n], func=AF.Square),
                            reads=[oT_b[u] for u in tiles_in], writes=[sq_b[p]])
                        for c in range(2):
                            S.op("pe", lambda c=c, p=p, n=n, bk=bk: nc.tensor.matmul(
                                banks[bk][:, 0:n], ones256, sq[p][:, c, 0:n], start=(c == 0), stop=(c == 1)),
                                reads=[sq_b[p], constb], writes=[bankb[bk]], inc=(c == 1))
                        S.op("act", lambda p=p, n=n, bk=bk: nc.scalar.activation(
                            out=tmp[p][:, 0:n], in_=banks[bk][:, 0:n], func=AF.Ln, bias=epsT[:, 0:1], scale=1.0),
                            reads=[bankb[bk], constb], writes=[tmp_b[p]])
                        S.op("act", lambda p=p, n=n, c0=c0: nc.scalar.activation(
                            out=rstd[:, c0:c0 + n], in_=tmp[p][:, 0:n], func=AF.Exp, scale=-0.5),
                            reads=[tmp_b[p]], writes=[rstd_b[ti]] + [Enb_b[u] for u in tiles_in])
                    def d_rproj(ti):
                        c0, n = TT[ti]
                        for c in range(2):
                            bk = (2 * ti + c) % 4
                            for kc in range(8):
                                S.op("pe", lambda kc=kc: nc.tensor.matmul(
                                    banks[bk][:, 0:n], wr_t[:, kc, c * 128:(c + 1) * 128],
                                    yn[:, kc, c0:c0 + n], start=(kc == 0), stop=(kc == 7)),
                                    reads=[wr_b, yn_b[ti]], writes=[bankb[bk]], inc=(kc == 7))

                    def d_rest(ti):
                        c0, n = TT[ti]
                        tiles_in = [16] if ti == 4 else [4 * ti + u for u in range(4)]
                        p = ti % 2
                        for c in range(2):
                            bk = (2 * ti + c) % 4
                            S.op("act", lambda c=c, bk=bk: nc.scalar.activation(
                                out=sr[p][:, c, 0:n], in_=banks[bk][:, 0:n], func=AF.Silu),
                                reads=[bankb[bk]], writes=[sr_b[p]])
                        for c in range(2):
                            S.op("dve", lambda c=c: nc.vector.scalar_tensor_tensor(
                                out=t1[p][:, c, 0:n], in0=oTt[:, c, c0:c0 + n],
                                scalar=gon[:, j * 8 + hd * 2 + c:j * 8 + hd * 2 + c + 1], in1=rstd[:, c0:c0 + n],
                                op0=ALU.mult, op1=ALU.mult),
                                reads=[oT_b[u] for u in tiles_in] + [rstd_b[ti], constb], writes=[t1_b[p]])
                        S.op("pool", lambda: nc.gpsimd.tensor_tensor(
                            out=og[p][:, :, 0:n], in0=t1[p][:, :, 0:n], in1=sr[p][:, :, 0:n], op=ALU.mult),
                            reads=[t1_b[p], sr_b[p]], writes=[og_b[p]])
                        for dc in range(8):
                            bk = 4 + (dc % 4)
                            for c in range(2):
                                S.op("pe", lambda c=c, dc=dc, bk=bk: nc.tensor.matmul(
                                    banks[bk][:, 0:n], wo_t[:, c, dc * 128:(dc + 1) * 128], og[p][:, c, 0:n],
                                    start=(c == 0), stop=(c == 1)),
                                    reads=[wo_b, og_b[p]], writes=[bankb[bk]], inc=(c == 1))
                            S.op("dve", lambda dc=dc, bk=bk: nc.vector.tensor_tensor(
                                out=hT[:, dc, c0:c0 + n], in0=hT[:, dc, c0:c0 + n], in1=banks[bk][:, 0:n], op=ALU.add),
                                reads=[bankb[bk], h_b[dc][ti]], writes=[h_b[dc][ti]])

                    d_rproj(0)
                    for ti in range(len(TT)):
                        if ti + 1 < len(TT):
                            d_rproj(ti + 1)
                        d_rest(ti)
                    if hd + 1 < 4:
                        load_ro(hd + 1)
                S.barrier()

        sub = 0
        if dbg == "gla":
            gla_mixer(0, 4)
            n_sub = 0
        for l in range(DEPTH):
            for part in range(3):
                if sub >= n_sub:
                    break
                if part == 0:
                    ffn(2 * l, 3 * l)
                elif part == 1:
                    if l % 2 == 0:
                        sb_mixer(l // 2, 3 * l + 1)
                    else:
                        gla_mixer(l // 2, 3 * l + 1)
                else:
                    ffn(2 * l + 1, 3 * l + 2)
                sub += 1

        with ExitStack() as ph:
            ys = [sb("ys%d" % i, [128, D], F32, ph) for i in range(2)]
            ys_b = [Buf(), Buf()]
            for i in range(16):
                s = i % 2
                c0 = i * 128
                for half in range(2):
                    bk = (2 * i + half) % 4
                    for q in range(4):
                        kc = half * 4 + q
                        S.op("pe", lambda bk=bk, q=q, kc=kc, c0=c0: nc.tensor.transpose(
                            out=banks[bk][:, q * 128:(q + 1) * 128], in_=hT[:, kc, c0:c0 + 128], identity=ident),
                            reads=[h_b[kc][i // 4], constb], writes=[bankb[bk]], inc=(q == 3))
                    if half == 0:
                        S.op("act", lambda bk=bk, s=s, half=half: nc.scalar.copy(
                            out=ys[s][:, half * 512:(half + 1) * 512], in_=banks[bk][:, :]),
                            reads=[bankb[bk]], writes=[ys_b[s]])
                    else:
                        S.op("dve", lambda bk=bk, s=s, half=half: nc.vector.tensor_copy(
                            out=ys[s][:, half * 512:(half + 1) * 512], in_=banks[bk][:, :]),
                            reads=[bankb[bk]], writes=[ys_b[s]])
                S.dma("sp", y_d[c0:c0 + 128, :], ys[s][:], d_out[s], reads=[ys_b[s]])
            S.barrier()
    return nc


def _consts():
    i = np.arange(128)
    cf = np.zeros((128, 4, 128), np.float32)
    cf[:, 0] = np.eye(128, dtype=np.float32)
    cf[:, 1] = -(1.0 / 16.0) * (i[:, None] <= i[None, :])
    cf[:, 2] = -(1.0 / 16.0) * (i[:, None] > i[None, :])
    cf[:, 3] = (i[:, None] <= i[None, :])
    cb = np.zeros((128, 9, 128), np.float32)
    cb[:, 0] = 1.0 / 1024.0
    cb[:, 1] = (i[:, None] // 64 == i[None, :] // 64) * (1.0 / 64.0)
    cb[:, 2] = 1.0 / 256.0
    cb[:, 3] = -1.0 * (i[:, None] >= i[None, :])
    cb[:, 4] = -1.0
    cb[:, 5] = np.eye(128, dtype=np.float32)
    cb[:, 6] = np.where(i[:, None] < i[None, :], 0.0, NEG)
    cb[:, 7] = cf[:, 1]
    cb[:, 8] = cf[:, 2]
    return cf.reshape(128, 512), cb.reshape(128, 1152)


def _layout(inp):
    f = lambda a: np.ascontiguousarray(np.asarray(a, dtype=np.float32))
    out = {}
    wgu = np.empty((2 * DEPTH, NFC, 128, 8, 256), np.float32)
    wd = np.empty((2 * DEPTH, 2, 8, 128, 11, 128), np.float32)
    for l in range(DEPTH):
        for wh, (kgu, kd) in enumerate((("ffn_a_w_gu", "ffn_a_w_down"), ("ffn_b_w_gu", "ffn_b_w_down"))):
            w = f(inp[kgu][l]).reshape(8, 128, 2, NFC, 128)
            wgu[2 * l + wh] = w.transpose(3, 1, 0, 2, 4).reshape(NFC, 128, 8, 256)
            w2 = f(inp[kd][l]).reshape(2, 11, 128, 8, 128)
            wd[2 * l + wh] = w2.transpose(0, 3, 2, 1, 4)
    out["wgu"] = wgu.reshape(2 * DEPTH, NFC, 128, 8 * 256)
    out["wd"] = wd.reshape(2 * DEPTH, 2, 8, 128, 11 * 128)
    wqkv = f(inp["sb_w_qkv"]).reshape(2, 8, 128, 3, 4, 256)
    out["wqkv"] = np.ascontiguousarray(wqkv.transpose(0, 4, 2, 1, 3, 5)).reshape(2, 4, 128, 8 * 768)
    wosb = f(inp["sb_w_o"]).reshape(2, 4, 2, 128, 1024)
    out["wosb"] = np.ascontiguousarray(wosb.transpose(0, 1, 3, 2, 4)).reshape(2, 4, 128, 2048)
    w_in = f(inp["gla_w_in"])
    win = np.empty((2, 4, 128, 8, 768), np.float32)
    for hd in range(4):
        cols = np.concatenate([np.arange(hd * 128, hd * 128 + 128), 512 + np.arange(hd * 128, hd * 128 + 128),
                               1024 + np.arange(hd * 256, hd * 256 + 256), 2048 + np.arange(hd * 256, hd * 256 + 256)])
        win[:, hd] = w_in[:, :, cols].reshape(2, 8, 128, 768).transpose(0, 2, 1, 3)
    out["win"] = np.ascontiguousarray(win[:, :, :, :, 0:512]).reshape(2, 4, 128, 8 * 512)
    out["winr"] = np.ascontiguousarray(win[:, :, :, :, 512:768]).reshape(2, 4, 128, 8 * 256)
    out["wgl"] = np.ascontiguousarray(w_in[:, :, 3072:3088].reshape(2, 8, 128, 16).transpose(0, 2, 1, 3)).reshape(2, 128, 128)
    out["wga"] = np.ascontiguousarray(np.concatenate([f(inp["gla_w_gate_up"]), f(inp["gla_b_gate"])[:, None, :]], axis=1))
    wogl = f(inp["gla_w_o"]).reshape(2, 4, 2, 128, 1024)
    out["wogl"] = np.ascontiguousarray(wogl.transpose(0, 1, 3, 2, 4)).reshape(2, 4, 128, 2048)
    norms = np.stack([f(inp["ffn_a_norm"]), f(inp["mix_norm"]), f(inp["ffn_b_norm"])], axis=1)
    out["norms"] = np.ascontiguousarray(norms.reshape(12, 8, 128).transpose(2, 0, 1)).reshape(128, 96)
    out["gon"] = np.ascontiguousarray(f(inp["gla_out_norm"]).reshape(2, 8, 128).transpose(2, 0, 1)).reshape(128, 16)
    gq = f(inp["sb_q_norm"])
    gk = f(inp["sb_k_norm"])
    gqk = np.stack([gq[0], gk[0], gq[1], gk[1]], axis=1)
    out["gqk"] = np.ascontiguousarray(np.concatenate([gqk, gqk], axis=0))
    out["meta"] = f(inp["meta"])
    cf, cb = _consts()
    out["cf"] = cf
    out["cb"] = cb
    return out


_NC_CACHE = {}


def kernel(**inputs):
    shared = _layout(inputs)
    x = np.ascontiguousarray(np.asarray(inputs["x"], dtype=np.float32))
    if "nc" not in _NC_CACHE:
        _NC_CACHE["nc"] = build_program()
    nc = _NC_CACHE["nc"]
    in_maps = []
    for b in range(8):
        m = dict(shared)
        m["x"] = x[b]
        in_maps.append(m)
    res = run_bass_kernel_spmd(nc, in_maps, core_ids=list(range(8)))
    return np.stack([np.asarray(r["y"], dtype=np.float32) for r in res.results], axis=0)
```

```python
from contextlib import ExitStack

import numpy as np
import concourse.bass as bass
import concourse.mybir as mybir
from concourse.bass_utils import run_bass_kernel_spmd

F32 = mybir.dt.float32
BF16 = mybir.dt.bfloat16
AF = mybir.ActivationFunctionType
ALU = mybir.AluOpType

D = 1024
SEQ = 2048
NMETA = 16
T = SEQ + NMETA
DFF = 2816
NFC = 22
DEPTH = 4
EPS = 1e-6
TT = [(0, 512), (512, 512), (1024, 512), (1536, 512), (2048, 16)]
TOKT = [(i * 128, 128) for i in range(16)] + [(2048, 16)]
NEG = -30000.0


class Buf:
    __slots__ = ("w", "r")

    def __init__(self):
        self.w = None
        self.r = {}


class Sched:
    def __init__(self, nc, st):
        self.nc = nc
        self.st = st
        self.E = {"pe": nc.tensor, "act": nc.scalar, "dve": nc.vector, "pool": nc.gpsimd, "sp": nc.sync}
        self.sems = {}
        self.cnt = {}
        for k in ("pe", "act", "dve", "pool"):
            self.sems[k] = st.enter_context(nc.semaphore("s_" + k))
            self.cnt[k] = 0
        self.seen = {e: {} for e in self.E}
        self.ndma = 0

    def dma_sem(self):
        k = "d%d" % self.ndma
        self.ndma += 1
        self.sems[k] = self.st.enter_context(self.nc.semaphore("s_" + k))
        self.cnt[k] = 0
        return k

    def wait(self, eng, toks):
        need = {}
        for t in toks:
            if t is None:
                continue
            k, v = t
            if eng == "pe" and k == "pe":
                continue
            if self.seen[eng].get(k, 0) >= v:
                continue
            if need.get(k, 0) < v:
                need[k] = v
        for k, v in need.items():
            self.E[eng].wait_ge(self.sems[k], v)
            self.seen[eng][k] = v

    @staticmethod
    def deps(reads, writes):
        toks = []
        for b in reads:
            toks.append(b.w)
        for b in writes:
            toks.append(b.w)
            toks.extend(b.r.items())
        return toks

    def _mark(self, tok, reads, writes):
        k, v = tok
        for b in reads:
            if b.r.get(k, 0) < v:
                b.r[k] = v
        for b in writes:
            b.w = tok
            b.r = {}

    def op(self, eng, fn, reads=(), writes=(), inc=True):
        self.wait(eng, self.deps(reads, writes))
        ins = fn()
        if inc:
            ins.then_inc(self.sems[eng], 1)
            self.cnt[eng] += 1
            tok = (eng, self.cnt[eng])
        else:
            tok = (eng, self.cnt[eng] + 1)
        self._mark(tok, reads, writes)
        return tok

    def dma(self, eng, out, in_, semk, reads=(), writes=()):
        self.wait(eng, self.deps(reads, writes))
        self.E[eng].dma_start(out=out, in_=in_).then_inc(self.sems[semk], 16)
        self.cnt[semk] += 16
        tok = (semk, self.cnt[semk])
        self._mark(tok, reads, writes)
        return tok

    def barrier(self):
        toks = [(k, v) for k, v in self.cnt.items() if v > 0]
        for e in self.E:
            self.wait(e, toks)


def build_program(n_sub=3 * DEPTH, dbg=False, gla_steps=4):
    nc = bass.Bass("TRN2", target_bir_lowering=False)
    dt = nc.dram_tensor
    x_d = dt("x", [SEQ, D], F32, kind="ExternalInput").ap()
    meta_d = dt("meta", [NMETA, D], F32, kind="ExternalInput").ap()
    wgu_d = dt("wgu", [2 * DEPTH, NFC, 128, 8 * 256], F32, kind="ExternalInput").ap()
    wd_d = dt("wd", [2 * DEPTH, 2, 8, 128, 11 * 128], F32, kind="ExternalInput").ap()
    wqkv_d = dt("wqkv", [2, 4, 128, 8 * 768], F32, kind="ExternalInput").ap()
    wosb_d = dt("wosb", [2, 4, 128, 2 * 1024], F32, kind="ExternalInput").ap()
    win_d = dt("win", [2, 4, 128, 8 * 512], F32, kind="ExternalInput").ap()
    winr_d = dt("winr", [2, 4, 128, 8 * 256], F32, kind="ExternalInput").ap()
    wgl_d = dt("wgl", [2, 128, 8 * 16], F32, kind="ExternalInput").ap()
    wga_d = dt("wga", [2, 17, 512], F32, kind="ExternalInput").ap()
    wogl_d = dt("wogl", [2, 4, 128, 2 * 1024], F32, kind="ExternalInput").ap()
    norms_d = dt("norms", [128, 12 * 8], F32, kind="ExternalInput").ap()
    gon_d = dt("gon", [128, 2 * 8], F32, kind="ExternalInput").ap()
    gqk_d = dt("gqk", [128, 4], F32, kind="ExternalInput").ap()
    cf_d = dt("cf", [128, 4 * 128], F32, kind="ExternalInput").ap()
    cb_d = dt("cb", [128, 9 * 128], F32, kind="ExternalInput").ap()
    y_d = dt("y", [SEQ, D], F32, kind="ExternalOutput").ap()
    if dbg:
        dq_d = dt("dq", [128, 2, T], BF16, kind="ExternalOutput").ap()
        dk_d = dt("dk", [128, 2, T], BF16, kind="ExternalOutput").ap()
        dv_d = dt("dv", [128, 17, 256], BF16, kind="ExternalOutput").ap()
        do_d = dt("do", [128, 2, T], BF16, kind="ExternalOutput").ap()

    with ExitStack() as st:
        S = Sched(nc, st)
        uniq = [0]

        def sb(name, shape, dtype, ctx=st):
            uniq[0] += 1
            return ctx.enter_context(nc.sbuf_tensor("%s_%d" % (name, uniq[0]), shape, dtype))
        hT = sb("hT", [128, 8, T], F32)
        yn = sb("yn", [128, 8, T], BF16)
        norms = sb("norms_s", [128, 96], F32)
        gon = sb("gon_s", [128, 16], F32)
        gqk = sb("gqk_s", [128, 4], F32)
        cf = sb("cf_s", [128, 4 * 128], F32)
        cb = sb("cb_s", [128, 9 * 128], BF16)
        epsT = sb("eps_s", [128, 1], F32)
        banks = [st.enter_context(nc.psum_tensor("bank%d" % i, [128, 512], F32)) for i in range(8)]
        bankb = [Buf() for _ in range(8)]
        ident = cf[:, 0:128]
        TIs = cb[:, 896:1024]
        SUs = cb[:, 1024:1152]
        mask01 = cf[:, 384:512]
        ones_mean = cb[:, 0:128]
        blk64 = cb[:, 128:256]
        ones256 = cb[:, 256:384]
        negtri = cb[:, 384:512]
        negones = cb[:, 512:640]
        identb = cb[:, 640:768]
        DM = cb[:, 768:896]
        constb = Buf()
        constb2 = Buf()
        h_b = [[Buf() for _ in range(5)] for _ in range(8)]
        yn_b = [Buf() for _ in range(5)]
        d_const = S.dma_sem()
        d_const2 = S.dma_sem()
        d_in = [S.dma_sem(), S.dma_sem()]
        d_out = [S.dma_sem(), S.dma_sem()]

        S.dma("sp", norms[:], norms_d, d_const, writes=[constb])
        S.dma("sp", gon[:], gon_d, d_const, writes=[constb])
        S.dma("sp", gqk[:], gqk_d, d_const, writes=[constb])
        S.dma("sp", cf[:], cf_d, d_const, writes=[constb])
        S.dma("pool", cb[:], cb_d, d_const2, writes=[constb2])
        S.op("dve", lambda: nc.vector.memset(epsT[:], EPS), writes=[constb])
        S.op("dve", lambda: nc.vector.tensor_scalar_mul(out=gqk[:, 0:1], in0=gqk[:, 0:1], scalar1=0.125),
             reads=[constb], writes=[constb])
        S.op("dve", lambda: nc.vector.tensor_scalar_mul(out=gqk[:, 2:3], in0=gqk[:, 2:3], scalar1=0.125),
             reads=[constb], writes=[constb])
        S.barrier()

        def tt_of_tok(i):
            return 4 if i == 16 else i // 4

        with ExitStack() as ph:
            xs = [sb("xs%d" % i, [128, D], F32, ph) for i in range(2)]
            xs_b = [Buf(), Buf()]
            for i, (c0, n) in enumerate(TOKT):
                s = i % 2
                src = x_d[c0:c0 + n, :] if i < 16 else meta_d
                S.dma("sp", xs[s][0:n, :], src, d_in[s], writes=[xs_b[s]])
                for half in range(2):
                    bk = (2 * i + half) % 4
                    pb = banks[bk]
                    for q in range(4):
                        kc = half * 4 + q
                        S.op("pe", lambda pb=pb, q=q, kc=kc, s=s, n=n: nc.tensor.transpose(
                            out=pb[:, q * 128:q * 128 + n], in_=xs[s][0:n, kc * 128:(kc + 1) * 128],
                            identity=ident[0:n, 0:n]),
                            reads=[xs_b[s], constb], writes=[bankb[bk]], inc=(q == 3))
                    src_ap = pb[:].rearrange("p (q c) -> p q c", q=4)[:, :, 0:n]
                    wr = [h_b[half * 4 + q][tt_of_tok(i)] for q in range(4)]
                    eng = "act" if half == 0 else "dve"
                    if eng == "act":
                        S.op("act", lambda src_ap=src_ap, half=half, c0=c0, n=n: nc.scalar.copy(
                            out=hT[:, half * 4:half * 4 + 4, c0:c0 + n], in_=src_ap),
                            reads=[bankb[bk]], writes=wr)
                    else:
                        S.op("dve", lambda src_ap=src_ap, half=half, c0=c0, n=n: nc.vector.tensor_copy(
                            out=hT[:, half * 4:half * 4 + 4, c0:c0 + n], in_=src_ap),
                            reads=[bankb[bk]], writes=wr)
            S.barrier()

        def rmsnorm(gidx, sq, sq_b, tmp, tmp_b, rstd, rstd_b):
            r = 0
            for ti, (c0, n) in enumerate(TT):
                bk = 6 + (ti % 2)
                for kc in range(8):
                    q = r % len(sq)
                    r += 1
                    S.op("act", lambda kc=kc, q=q, c0=c0, n=n: nc.scalar.activation(
                        out=sq[q][:, 0:n], in_=hT[:, kc, c0:c0 + n], func=AF.Square),
                        reads=[h_b[kc][ti]], writes=[sq_b[q]])
                    S.op("pe", lambda kc=kc, q=q, n=n, bk=bk: nc.tensor.matmul(
                        banks[bk][:, 0:n], ones_mean, sq[q][:, 0:n], start=(kc == 0), stop=(kc == 7)),
                        reads=[sq_b[q], constb2], writes=[bankb[bk]])
                pt = ti % len(tmp)
                pr = ti % len(rstd)
                S.op("act", lambda pt=pt, n=n, bk=bk: nc.scalar.activation(
                    out=tmp[pt][:, 0:n], in_=banks[bk][:, 0:n], func=AF.Ln, bias=epsT[:, 0:1], scale=1.0),
                    reads=[bankb[bk], constb], writes=[tmp_b[pt]])
                S.op("act", lambda pt=pt, pr=pr, n=n: nc.scalar.activation(
                    out=rstd[pr][:, 0:n], in_=tmp[pt][:, 0:n], func=AF.Exp, scale=-0.5),
                    reads=[tmp_b[pt]], writes=[rstd_b[pr]])
                for kc in range(8):
                    S.op("dve", lambda kc=kc, pr=pr, c0=c0, n=n: nc.vector.scalar_tensor_tensor(
                        out=yn[:, kc, c0:c0 + n], in0=hT[:, kc, c0:c0 + n],
                        scalar=norms[:, gidx * 8 + kc:gidx * 8 + kc + 1], in1=rstd[pr][:, 0:n],
                        op0=ALU.mult, op1=ALU.mult),
                        reads=[h_b[kc][ti], rstd_b[pr], constb], writes=[yn_b[ti]])

        def ffn(fidx, gidx):
            with ExitStack() as ph:
                n_sq = [sb("f_nsq%d" % i, [128, 512], BF16, ph) for i in range(4)]
                n_tmp = [sb("f_ntmp%d" % i, [128, 512], F32, ph) for i in range(2)]
                n_rs = [sb("f_nrs%d" % i, [128, 512], F32, ph) for i in range(2)]
                act = sb("f_act", [128, 11, T], BF16, ph)
                NGS = 5
                wgu = [sb("f_wgu%d" % i, [128, 8, 256], BF16, ph) for i in range(NGS)]
                wdn = [sb("f_wd%d" % i, [128, 11, 128], BF16, ph) for i in range(3)]
                sg = [sb("f_sg%d" % i, [128, 512], F32, ph) for i in range(2)]
                act_b = [[Buf() for _ in range(5)] for _ in range(11)]
                wgu_b = [Buf() for _ in range(NGS)]
                wdn_b = [Buf() for _ in range(3)]
                sg_b = [Buf(), Buf()]
                wgu_s = [S.dma_sem() for _ in range(NGS)]
                wdn_s = [S.dma_sem() for _ in range(3)]
                steps = []
                for grp in range(2):
                    for fci in range(11):
                        steps.append(("gu", grp, fci))
                    for dc in range(8):
                        steps.append(("d", grp, dc))
                kcount = {"gu": 0, "d": 0}
                slot_of = []
                for s_ in steps:
                    slot_of.append(kcount[s_[0]] % (NGS if s_[0] == "gu" else 3))
                    kcount[s_[0]] += 1
                loaded = [0]

                def load(m):
                    kind, grp, idx = steps[m]
                    s = slot_of[m]
                    if kind == "gu":
                        fc = grp * 11 + idx
                        S.dma("pool", wgu[s][:].rearrange("p a b -> p (a b)"), wgu_d[fidx, fc],
                              wgu_s[s], writes=[wgu_b[s]])
                    else:
                        S.dma("pool", wdn[s][:].rearrange("p a b -> p (a b)"), wd_d[fidx, grp, idx],
                              wdn_s[s], writes=[wdn_b[s]])

                for m0 in range(NGS):
                    load(m0)
                loaded[0] = NGS
                rmsnorm(gidx, n_sq, [Buf() for _ in n_sq], n_tmp, [Buf() for _ in n_tmp], n_rs, [Buf() for _ in n_rs])
                gstate = [0]

                def gu_tile(m, ti):
                    kind, grp, fci = steps[m]
                    s = slot_of[m]
                    c0, n = TT[ti]
                    p = gstate[0] % 2
                    gstate[0] += 1
                    bg, bu = p, 2 + p
                    for kc in range(8):
                        S.op("pe", lambda kc=kc: nc.tensor.matmul(
                            banks[bg][:, 0:n], wgu[s][:, kc, 0:128], yn[:, kc, c0:c0 + n],
                            start=(kc == 0), stop=(kc == 7)),
                            reads=[wgu_b[s], yn_b[ti]], writes=[bankb[bg]], inc=(kc == 7))
                    for kc in range(8):
                        S.op("pe", lambda kc=kc: nc.tensor.matmul(
                            banks[bu][:, 0:n], wgu[s][:, kc, 128:256], yn[:, kc, c0:c0 + n],
                            start=(kc == 0), stop=(kc == 7)),
                            reads=[wgu_b[s], yn_b[ti]], writes=[bankb[bu]], inc=(kc == 7))
                    S.op("act", lambda: nc.scalar.activation(
                        out=sg[p][:, 0:n], in_=banks[bg][:, 0:n], func=AF.Silu),
                        reads=[bankb[bg]], writes=[sg_b[p]])
                    S.op("dve", lambda: nc.vector.tensor_tensor(
                        out=act[:, fci, c0:c0 + n], in0=sg[p][:, 0:n], in1=banks[bu][:, 0:n], op=ALU.mult),
                        reads=[sg_b[p], bankb[bu]], writes=[act_b[fci][ti]])

                for ti in range(len(TT)):
                    for m in range(3):
                        gu_tile(m, ti)
                gcount = 0
                for m, (kind, grp, idx) in enumerate(steps):
                    if m < 3:
                        continue
                    while loaded[0] < len(steps) and loaded[0] <= m + 2:
                        load(loaded[0])
                        loaded[0] += 1
                    s = slot_of[m]
                    if kind == "gu":
                        for ti in range(len(TT)):
                            gu_tile(m, ti)
                    else:
                        dc = idx
                        for ti, (c0, n) in enumerate(TT):
                            p = gcount % 2
                            gcount += 1
                            bo = 4 + p
                            for fci in range(11):
                                S.op("pe", lambda fci=fci, s=s, c0=c0, n=n, bo=bo: nc.tensor.matmul(
                                    banks[bo][:, 0:n], wdn[s][:, fci, :], act[:, fci, c0:c0 + n],
                                    start=(fci == 0), stop=(fci == 10)),
                                    reads=[wdn_b[s], act_b[fci][ti]], writes=[bankb[bo]], inc=(fci == 10))
                            S.op("dve", lambda dc=dc, c0=c0, n=n, bo=bo: nc.vector.scalar_tensor_tensor(
                                out=hT[:, dc, c0:c0 + n], in0=banks[bo][:, 0:n], scalar=0.5,
                                in1=hT[:, dc, c0:c0 + n], op0=ALU.mult, op1=ALU.add),
                                reads=[bankb[bo], h_b[dc][ti]], writes=[h_b[dc][ti]])
                S.barrier()

        def rstd_from(bk, np_, n, tmp_t, tmp_b, out_t, out_b):
            S.op("act", lambda: nc.scalar.activation(
                out=tmp_t[0:np_, 0:n], in_=banks[bk][0:np_, 0:n], func=AF.Ln, bias=epsT[0:np_, 0:1], scale=1.0),
                reads=[bankb[bk], constb], writes=[tmp_b])
            S.op("act", lambda: nc.scalar.activation(
                out=out_t[0:np_, 0:n], in_=tmp_t[0:np_, 0:n], func=AF.Exp, scale=-0.5),
                reads=[tmp_b], writes=[out_b])

        def sb_mixer(j, gidx):
            with ExitStack() as ph:
                wq = [sb("s_wq%d" % i, [128, 8, 768], BF16, ph) for i in range(2)]
                wo = [sb("s_wo%d" % i, [128, 2, 1024], BF16, ph) for i in range(2)]
                qT = sb("s_qT", [128, 2, T], BF16, ph)
                kT = sb("s_kT", [128, 2, T], BF16, ph)
                vg = sb("s_v", [128, 17, 256], BF16, ph)
                oT = sb("s_oT", [128, 2, T], BF16, ph)
                sq = [sb("s_sq%d" % i, [128, 512], BF16, ph) for i in range(2)]
                tmp = [sb("s_tmp%d" % i, [128, 512], F32, ph) for i in range(2)]
                rs = [sb("s_rs%d" % i, [128, 512], F32, ph) for i in range(2)]
                et = [sb("s_e%d" % i, [128, 512], F32, ph) for i in range(4)]
                spt = [sb("s_sp%d" % i, [128, 512], BF16, ph) for i in range(4)]
                wt = [sb("s_w%d" % i, [128, 512], BF16, ph) for i in range(4)]
                A32 = [sb("s_A32_%d" % i, [128, 512], F32, ph) for i in range(2)]
                Abf = [sb("s_Abf_%d" % i, [128, 512], BF16, ph) for i in range(2)]
                wq_b = [Buf(), Buf()]
                wo_b = [Buf(), Buf()]
                wq_s = [S.dma_sem(), S.dma_sem()]
                wo_s = [S.dma_sem(), S.dma_sem()]
                qT_b = [[Buf() for _ in range(5)] for _ in range(2)]
                kT_b = [[Buf() for _ in range(5)] for _ in range(2)]
                v_b = [Buf() for _ in range(17)]
                oT_b = [[Buf() for _ in range(5)] for _ in range(2)]
                sq_b = [Buf(), Buf()]
                tmp_b = [Buf(), Buf()]
                rs_b = [Buf(), Buf()]
                et_b = [Buf() for _ in range(4)]
                sp_b = [Buf() for _ in range(4)]
                w_b = [Buf() for _ in range(4)]
                A32_b = [Buf(), Buf()]
                Abf_b = [Buf(), Buf()]
                rmsnorm(gidx, sq, sq_b, tmp, tmp_b, rs, rs_b)

                def loadw(hg):
                    s = hg % 2
                    S.dma("pool", wq[s][:].rearrange("p a b -> p (a b)"), wqkv_d[j, hg], wq_s[s], writes=[wq_b[s]])
                    S.dma("pool", wo[s][:].rearrange("p a b -> p (a b)"), wosb_d[j, hg], wo_s[s], writes=[wo_b[s]])

                loadw(0)
                pcount = 0
                for hg in range(4):
                    if hg + 1 < 4:
                        loadw(hg + 1)
                    ws = hg % 2
                    ptiles = []
                    for which, dst, dst_b, gcol in ((0, qT, qT_b, 2 * j), (1, kT, kT_b, 2 * j + 1)):
                        for c in range(2):
                            for ti, (c0, n) in enumerate(TT):
                                ptiles.append((which, dst, dst_b, gcol, c, ti, c0, n))

                    def p_stage1(t):
                        which, dst, dst_b, gcol, c, ti, c0, n = ptiles[t]
                        bk = 2 * (t % 4)
                        col = which * 256 + c * 128
                        for kc in range(8):
                            S.op("pe", lambda kc=kc: nc.tensor.matmul(
                                banks[bk][:, 0:n], wq[ws][:, kc, col:col + 128], yn[:, kc, c0:c0 + n],
                                start=(kc == 0), stop=(kc == 7)),
                                reads=[wq_b[ws], yn_b[ti]], writes=[bankb[bk]], inc=(kc == 7))

                    def p_stage2(t):
                        which, dst, dst_b, gcol, c, ti, c0, n = ptiles[t]
                        bk = 2 * (t % 4)
                        bk2 = bk + 1
                        p = t % 2
                        S.op("act", lambda: nc.scalar.activation(
                            out=sq[p][:, 0:n], in_=banks[bk][:, 0:n], func=AF.Square),
                            reads=[bankb[bk]], writes=[sq_b[p]])
                        S.op("pe", lambda: nc.tensor.matmul(
                            banks[bk2][:, 0:n], blk64, sq[p][:, 0:n], start=True, stop=True),
                            reads=[sq_b[p], constb], writes=[bankb[bk2]])
                        rstd_from(bk2, 128, n, tmp[p], tmp_b[p], rs[p], rs_b[p])
                        S.op("dve", lambda: nc.vector.scalar_tensor_tensor(
                            out=dst[:, c, c0:c0 + n], in0=banks[bk][:, 0:n], scalar=gqk[:, gcol:gcol + 1],
                            in1=rs[p][:, 0:n], op0=ALU.mult, op1=ALU.mult),
                            reads=[bankb[bk], rs_b[p], constb], writes=[dst_b[c][ti]])

                    p_stage1(0)
                    for t in range(len(ptiles)):
                        if t + 1 < len(ptiles):
                            p_stage1(t + 1)
                        p_stage2(t)
                    for i, (c0, n) in enumerate(TOKT):
                        bk = pcount % 8
                        pcount += 1
                        for kc in range(8):
                            S.op("pe", lambda kc=kc, c0=c0, n=n, bk=bk: nc.tensor.matmul(
                                banks[bk][0:n, 0:256], yn[:, kc, c0:c0 + n], wq[ws][:, kc, 512:768],
                                start=(kc == 0), stop=(kc == 7)),
                                reads=[wq_b[ws], yn_b[tt_of_tok(i)]], writes=[bankb[bk]], inc=(kc == 7))
                        if i % 2 == 0:
                            S.op("act", lambda i=i, n=n, bk=bk: nc.scalar.copy(
                                out=vg[0:n, i, :], in_=banks[bk][0:n, 0:256]), reads=[bankb[bk]], writes=[v_b[i]])
                        else:
                            S.op("dve", lambda i=i, n=n, bk=bk: nc.vector.tensor_copy(
                                out=vg[0:n, i, :], in_=banks[bk][0:n, 0:256]), reads=[bankb[bk]], writes=[v_b[i]])
                    batches = []
                    for c in range(2):
                        for q4 in range(4):
                            for Sb in range(4 * q4 + 3, -1, -1):
                                batches.append((c, q4, Sb))
                            batches.append((c, q4, 16))
                        batches.append((c, 4, 16))
                    ocount = [0]
                    o_bank_of = {}

                    def geom(b):
                        c, q4, Sb = b
                        if q4 == 4:
                            return dict(c=c, q4=4, Sb=16, nk=16, k0=2048, qc0=2048, lc0=0, n=16, diag=True,
                                        first=True, last=True, useA=False)
                        if Sb == 16:
                            return dict(c=c, q4=q4, Sb=16, nk=16, k0=2048, qc0=512 * q4, lc0=0, n=512, diag=False,
                                        first=False, last=True, useA=True)
                        lc0 = max(0, Sb - 4 * q4) * 128
                        return dict(c=c, q4=q4, Sb=Sb, nk=128, k0=Sb * 128, qc0=512 * q4 + lc0, lc0=lc0, n=512 - lc0,
                                    diag=(Sb >= 4 * q4), first=(Sb == 4 * q4 + 3), last=False,
                                    useA=(Sb != 4 * q4 + 3))

                    def zmm(bi, g, hh):
                        pbase = hh * 64
                        zb = (bi % 2) * 2 + hh
                        nk, n = g["nk"], g["n"]
                        ti_q = g["q4"]
                        ti_k = tt_of_tok(g["Sb"])
                        kap = kT[pbase:pbase + 64, g["c"], g["k0"]:g["k0"] + nk]
                        rd = [kT_b[g["c"]][ti_k], qT_b[g["c"]][ti_q]]
                        S.op("pe", lambda: nc.tensor.matmul(
                            banks[zb][0:nk, 0:n], kap, qT[pbase:pbase + 64, g["c"], g["qc0"]:g["qc0"] + n],
                            start=True, stop=False), reads=rd, writes=[bankb[zb]], inc=(not g["diag"]))
                        if g["diag"]:
                            nd = min(128, n)
                            S.op("pe", lambda: nc.tensor.matmul(
                                banks[zb][0:nk, 0:nd], identb[0:nk, 0:nk], DM[0:nk, 0:nd], start=False, stop=False),
                                reads=[constb], writes=[bankb[zb]])

                    def el(bi, g, hh):
                        zb = (bi % 2) * 2 + hh
                        ei = (2 * bi + hh) % 4
                        nk, n = g["nk"], g["n"]
                        eb = 4 + (ei % 2)
                        S.op("act", lambda: nc.scalar.activation(
                            out=banks[eb][0:nk, 0:n], in_=banks[zb][0:nk, 0:n], func=AF.Exp),
                            reads=[bankb[zb]], writes=[bankb[eb]])
                        S.op("act", lambda: nc.scalar.activation(
                            out=spt[ei][0:nk, 0:n], in_=banks[eb][0:nk, 0:n], func=AF.Ln, bias=1.0, scale=1.0),
                            reads=[bankb[eb]], writes=[sp_b[ei]])

                    def ta(bi, g, hh):
                        zb = (bi % 2) * 2 + hh
                        ei = (2 * bi + hh) % 4
                        nk, n, lc0 = g["nk"], g["n"], g["lc0"]
                        S.op("pe", lambda: nc.tensor.matmul(
                            banks[zb][0:nk, 0:n], negtri[0:nk, 0:nk], spt[ei][0:nk, 0:n], start=False,
                            stop=(not g["useA"])), reads=[sp_b[ei], constb], writes=[bankb[zb]],
                            inc=(not g["useA"]))
                        if g["useA"]:
                            S.op("pe", lambda: nc.tensor.matmul(
                                banks[zb][0:nk, 0:n], negones[:, 0:nk], Abf[hh][:, lc0:lc0 + n], start=False, stop=True),
                                reads=[Abf_b[hh], constb], writes=[bankb[zb]])

                    def aupd(bi, g, hh):
                        if g["last"]:
                            return
                        zb = (bi % 2) * 2 + hh
                        ei = (2 * bi + hh) % 4
                        n, lc0 = g["n"], g["lc0"]
                        if g["first"]:
                            S.op("pool", lambda: nc.gpsimd.memset(A32[hh][:], 0.0), writes=[A32_b[hh]])
                        S.op("pool", lambda: nc.gpsimd.tensor_tensor(
                            out=A32[hh][:, lc0:lc0 + n], in0=A32[hh][:, lc0:lc0 + n], in1=spt[ei][:, 0:n], op=ALU.add),
                            reads=[sp_b[ei], A32_b[hh]], writes=[A32_b[hh]])
                        if g["first"]:
                            S.op("dve", lambda: nc.vector.tensor_copy(out=Abf[hh][:], in_=A32[hh][:]),
                                 reads=[A32_b[hh]], writes=[Abf_b[hh]])
                        else:
                            S.op("dve", lambda: nc.vector.tensor_copy(out=Abf[hh][:, lc0:lc0 + n],
                                                                      in_=A32[hh][:, lc0:lc0 + n]),
                                 reads=[A32_b[hh]], writes=[Abf_b[hh]])

                    def ea(bi, g, hh):
                        zb = (bi % 2) * 2 + hh
                        ei = (2 * bi + hh) % 4
                        nk, n = g["nk"], g["n"]
                        S.op("act", lambda: nc.scalar.activation(
                            out=wt[ei][0:nk, 0:n], in_=banks[zb][0:nk, 0:n], func=AF.Exp),
                            reads=[bankb[zb]], writes=[w_b[ei]])

                    def pv(bi, g, hh):
                        zb = (bi % 2) * 2 + hh
                        ei = (2 * bi + hh) % 4
                        pbase = hh * 64
                        nk, n, lc0 = g["nk"], g["n"], g["lc0"]
                        key = (g["c"], g["q4"])
                        if key not in o_bank_of:
                            o_bank_of[key] = 6 + (ocount[0] % 2)
                            ocount[0] += 1
                        ob = o_bank_of[key]
                        vi = g["Sb"]
                        vap = vg[0:nk, vi, (2 * g["c"] + hh) * 64:(2 * g["c"] + hh) * 64 + 64]
                        tp = (0, pbase)
                        rd = [v_b[vi], w_b[ei]]
                        S.op("pe", lambda: nc.tensor.matmul(
                            banks[ob][pbase:pbase + 64, lc0:lc0 + n], vap, wt[ei][0:nk, 0:n], start=g["first"],
                            stop=g["last"], tile_position=tp), reads=rd, writes=[bankb[ob]])

                    def oevac(g):
                        key = (g["c"], g["q4"])
                        ob = o_bank_of[key]
                        n = 16 if g["q4"] == 4 else 512
                        c0 = 2048 if g["q4"] == 4 else 512 * g["q4"]
                        S.op("dve", lambda: nc.vector.tensor_copy(out=oT[:, g["c"], c0:c0 + n], in_=banks[ob][:, 0:n]),
                             reads=[bankb[ob]], writes=[oT_b[g["c"]][g["q4"]]])

                    geoms = [geom(b) for b in batches]
                    nb = len(geoms)
                    for hh in range(2):
                        zmm(0, geoms[0], hh)
                    for k in range(2 * nb):
                        bi, hh = divmod(k, 2)
                        g = geoms[bi]
                        el(bi, g, hh)
                        if k >= 1:
                            pb_, ph_ = divmod(k - 1, 2)
                            ea(pb_, geoms[pb_], ph_)
                        ta(bi, g, hh)
                        aupd(bi, g, hh)
                        if hh == 0 and bi + 1 < nb:
                            for h2 in range(2):
                                zmm(bi + 1, geoms[bi + 1], h2)
                        if k >= 1:
                            pv(pb_, geoms[pb_], ph_)
                            if ph_ == 1 and geoms[pb_]["last"]:
                                oevac(geoms[pb_])
                    ea(nb - 1, geoms[nb - 1], 1)
                    pv(nb - 1, geoms[nb - 1], 1)
                    oevac(geoms[nb - 1])
                    if dbg and hg == 0:
                        dsem = S.dma_sem()
                        S.dma("sp", dq_d, qT[:], dsem, reads=[b for r in qT_b for b in r])
                        S.dma("sp", dk_d, kT[:], dsem, reads=[b for r in kT_b for b in r])
                        S.dma("sp", dv_d, vg[:], dsem, reads=v_b)
                        S.dma("sp", do_d, oT[:], dsem, reads=[b for r in oT_b for b in r])
                    for dc in range(8):
                        for ti, (c0, n) in enumerate(TT):
                            bk = pcount % 8
                            pcount += 1
                            for c in range(2):
                                S.op("pe", lambda c=c, dc=dc, c0=c0, n=n, bk=bk: nc.tensor.matmul(
                                    banks[bk][:, 0:n], wo[ws][:, c, dc * 128:(dc + 1) * 128], oT[:, c, c0:c0 + n],
                                    start=(c == 0), stop=(c == 1)),
                                    reads=[wo_b[ws], oT_b[c][ti]], writes=[bankb[bk]], inc=(c == 1))
                            S.op("dve", lambda dc=dc, c0=c0, n=n, bk=bk: nc.vector.tensor_tensor(
                                out=hT[:, dc, c0:c0 + n], in0=hT[:, dc, c0:c0 + n], in1=banks[bk][:, 0:n], op=ALU.add),
                                reads=[bankb[bk], h_b[dc][ti]], writes=[h_b[dc][ti]])
                S.barrier()

        def gla_mixer(j, gidx):
            with ExitStack() as ph:
                wqkv_t = sb("g_wqkv", [128, 8, 512], BF16, ph)
                wr_t = sb("g_wr", [128, 8, 256], BF16, ph)
                wo_t = sb("g_wo", [128, 2, 1024], BF16, ph)
                wgl = sb("g_wgl", [128, 8, 16], BF16, ph)
                wga = sb("g_wga", [32, 512], BF16, ph)
                glT = sb("g_glT", [32, T], BF16, ph)
                qd = sb("g_qd", [128, T], BF16, ph)
                kd = sb("g_kd", [128, T], BF16, ph)
                kst = sb("g_kst", [128, 17, 128], BF16, ph)
                vt = sb("g_vt", [128, 17, 256], BF16, ph)
                Eb = sb("g_Eb", [128, T], F32, ph)
                Enb = sb("g_Enb", [128, T], F32, ph)
                Eaft = sb("g_Eaft", [128, 17, 128], F32, ph)
                et = [sb("g_e%d" % i, [128, 128], F32, ph) for i in range(2)]
                spt = [sb("g_sp%d" % i, [128, 128], F32, ph) for i in range(2)]
                shi = [sb("g_shi%d" % i, [128, 128], BF16, ph) for i in range(2)]
                slo = [sb("g_slo%d" % i, [128, 128], BF16, ph) for i in range(2)]
                shi_b = [Buf(), Buf()]
                slo_b = [Buf(), Buf()]
                oTt = sb("g_oT", [128, 2, T], F32, ph)
                rstd = Enb
                tmp = [sb("g_tmp", [128, 512], F32, ph)] * 2
                sq = [sb("g_sq", [128, 2, 512], BF16, ph)] * 2
                sr = [sb("g_sr", [128, 2, 512], BF16, ph)] * 2
                t1 = [sb("g_t1", [128, 2, 512], F32, ph)] * 2
                og = [sb("g_og%d" % i, [128, 2, 512], BF16, ph) for i in range(2)]
                S32 = sb("g_S32", [128, 256], F32, ph)
                Sbf2 = [sb("g_Sbf%d" % i, [128, 256], BF16, ph) for i in range(2)]
                attm = [sb("g_att%d" % i, [128, 128], BF16, ph) for i in range(2)]
                wqkv_b, wr_b, wo_b = Buf(), Buf(), Buf()
                wqkv_s, wr_s, wo_s = S.dma_sem(), S.dma_sem(), S.dma_sem()
                wg_s = S.dma_sem()
                wg_b = Buf()
                glT_b = [Buf() for _ in range(5)]
                qd_b = [Buf() for _ in range(5)]
                kd_b = [Buf() for _ in range(5)]
                kst_b = [Buf() for _ in range(17)]
                vt_b = [Buf() for _ in range(17)]
                Eb_b = [Buf() for _ in range(17)]
                Enb_b = [Buf() for _ in range(17)]
                Eaft_b = [Buf() for _ in range(17)]
                et_b = [Buf(), Buf()]
                sp_b = [Buf(), Buf()]
                oT_b = [Buf() for _ in range(17)]
                rstd_b = [Buf() for _ in range(5)]
                tmp_b = [Buf()] * 2
                sq_b = [Buf()] * 2
                sr_b = [Buf()] * 2
                t1_b = [Buf()] * 2
                og_b = [Buf(), Buf()]
                S32_b = Buf()
                Sbf2_b = [Buf(), Buf()]
                att_b = [Buf(), Buf()]
                rmsnorm(gidx, [sq[0][:, 0, :], sq[0][:, 1, :]], [Buf(), Buf()], [tmp[0]], [tmp_b[0]],
                        [t1[0][:, 0, :], t1[0][:, 1, :]], [Buf(), Buf()])
                S.barrier()

                def load_qkv(hd):
                    S.dma("pool", wqkv_t[:].rearrange("p a b -> p (a b)"), win_d[j, hd], wqkv_s, writes=[wqkv_b])

                def load_ro(hd):
                    S.dma("pool", wr_t[:].rearrange("p a b -> p (a b)"), winr_d[j, hd], wr_s, writes=[wr_b])
                    S.dma("pool", wo_t[:].rearrange("p a b -> p (a b)"), wogl_d[j, hd], wo_s, writes=[wo_b])

                S.dma("pool", wgl[:].rearrange("p a b -> p (a b)"), wgl_d[j], wg_s, writes=[wg_b])
                S.op("pool", lambda: nc.gpsimd.memset(wga[:], 0.0), writes=[wg_b])
                S.dma("pool", wga[0:17, :], wga_d[j], wg_s, writes=[wg_b])
                load_qkv(0)
                load_ro(0)
                S.op("dve", lambda: nc.vector.memset(glT[:], 1.0), writes=glT_b)
                pcount = 0
                for ti, (c0, n) in enumerate(TT):
                    p = pcount % 2
                    pcount += 1
                    bk = 6 + p
                    for kc in range(8):
                        S.op("pe", lambda kc=kc, c0=c0, n=n, bk=bk: nc.tensor.matmul(
                            banks[bk][0:16, 0:n], wgl[:, kc, :], yn[:, kc, c0:c0 + n], start=(kc == 0), stop=(kc == 7)),
                            reads=[wg_b, yn_b[ti]], writes=[bankb[bk]], inc=(kc == 7))
                    S.op("dve", lambda c0=c0, n=n, bk=bk: nc.vector.tensor_copy(
                        out=glT[0:16, c0:c0 + n], in_=banks[bk][0:16, 0:n]),
                        reads=[bankb[bk]], writes=[glT_b[ti]])

                for hd in range(4):
                    for gi, (c0g, ng) in enumerate(TT):
                        tiles = [16] if gi == 4 else [4 * gi + u for u in range(4)]
                        L = len(tiles)
                        n = TOKT[tiles[0]][1]
                        W = L * 128 if n == 128 else n
                        p = gi % 2
                        b0, b1, b2 = 0 + p, 2 + p, 4 + p
                        spv = t1[0][:, p, :]
                        hi = sq[0][:, 0, :] if p == 0 else og[0][:, 0, :]
                        lo = sq[0][:, 1, :] if p == 0 else og[0][:, 1, :]
                        hl_b = sq_b[0] if p == 0 else og_b[0]
                        for u, ti_ in enumerate(tiles):
                            c0 = TOKT[ti_][0]
                            S.op("pe", lambda c0=c0, n=n, b0=b0, u=u: nc.tensor.matmul(
                                banks[b0][0:n, u * 128:(u + 1) * 128], glT[0:32, c0:c0 + n],
                                wga[0:32, hd * 128:(hd + 1) * 128], start=True, stop=True),
                                reads=[glT_b[gi], wg_b], writes=[bankb[b0]], inc=(u == L - 1))
                        Wg = L * 128
                        S.op("act", lambda n=n, b0=b0, Wg=Wg: nc.scalar.activation(
                            out=tmp[0][0:n, 0:Wg], in_=banks[b0][0:n, 0:Wg], func=AF.Exp, scale=-1.0),
                            reads=[bankb[b0]], writes=[tmp_b[0]])
                        S.op("act", lambda n=n, Wg=Wg, spv=spv: nc.scalar.activation(
                            out=spv[0:n, 0:Wg], in_=tmp[0][0:n, 0:Wg], func=AF.Ln, bias=1.0, scale=1.0),
                            reads=[tmp_b[0]], writes=[t1_b[0]])
                        S.op("dve", lambda n=n, Wg=Wg, spv=spv, hi=hi: nc.vector.tensor_copy(
                            out=hi[0:n, 0:Wg], in_=spv[0:n, 0:Wg]), reads=[t1_b[0]], writes=[hl_b])
                        S.op("dve", lambda n=n, Wg=Wg, spv=spv, hi=hi, lo=lo: nc.vector.tensor_tensor(
                            out=lo[0:n, 0:Wg], in0=spv[0:n, 0:Wg], in1=hi[0:n, 0:Wg], op=ALU.subtract),
                            reads=[t1_b[0], hl_b], writes=[hl_b])
                        for u in range(L):
                            for part, src in enumerate((hi, lo)):
                                S.op("pe", lambda n=n, b1=b1, u=u, src=src, part=part: nc.tensor.matmul(
                                    banks[b1][:, u * 128:u * 128 + n], src[0:n, u * 128:(u + 1) * 128], TIs[0:n, 0:n],
                                    start=(part == 0), stop=(part == 1)),
                                    reads=[hl_b, constb], writes=[bankb[b1]], inc=(u == L - 1 and part == 1))
                        for u in range(L):
                            for part, src in enumerate((hi, lo)):
                                S.op("pe", lambda n=n, b2=b2, u=u, src=src, part=part: nc.tensor.matmul(
                                    banks[b2][0:n, u * 128:(u + 1) * 128], SUs[0:n, 0:n], src[0:n, u * 128:(u + 1) * 128],
                                    start=(part == 0), stop=(part == 1)),
                                    reads=[hl_b, constb], writes=[bankb[b2]], inc=(u == L - 1 and part == 1))
                        S.op("act", lambda c0g=c0g, ng=ng, b1=b1: nc.scalar.activation(
                            out=Eb[:, c0g:c0g + ng], in_=banks[b1][:, 0:ng], func=AF.Exp),
                            reads=[bankb[b1]], writes=[Eb_b[u_] for u_ in tiles])
                        S.op("act", lambda c0g=c0g, ng=ng, b1=b1: nc.scalar.activation(
                            out=Enb[:, c0g:c0g + ng], in_=banks[b1][:, 0:ng], func=AF.Exp, scale=-1.0),
                            reads=[bankb[b1]], writes=[Enb_b[u_] for u_ in tiles] + [rstd_b[gi]])
                        t0_ = tiles[0]
                        S.op("act", lambda n=n, b2=b2, Wg=Wg, t0_=t0_, L=L: nc.scalar.activation(
                            out=Eaft[0:n, t0_:t0_ + L, :],
                            in_=banks[b2][0:n, 0:Wg].rearrange("p (l k) -> p l k", l=L), func=AF.Exp),
                            reads=[bankb[b2]], writes=[Eaft_b[u_] for u_ in tiles])
                    if gla_steps < 2:
                        continue
                    for ti, (c0, n) in enumerate(TT):
                        tiles_in = [16] if ti == 4 else [4 * ti + u for u in range(4)]
                        r4 = pcount % 4
                        pcount += 1
                        bq, bkk = 2 * r4, 2 * r4 + 1
                        for kc in range(8):
                            S.op("pe", lambda kc=kc, c0=c0, n=n, bq=bq: nc.tensor.matmul(
                                banks[bq][:, 0:n], wqkv_t[:, kc, 0:128], yn[:, kc, c0:c0 + n],
                                start=(kc == 0), stop=(kc == 7)),
                                reads=[wqkv_b, yn_b[ti]], writes=[bankb[bq]], inc=(kc == 7))
                        for kc in range(8):
                            S.op("pe", lambda kc=kc, c0=c0, n=n, bkk=bkk: nc.tensor.matmul(
                                banks[bkk][:, 0:n], wqkv_t[:, kc, 128:256], yn[:, kc, c0:c0 + n],
                                start=(kc == 0), stop=(kc == 7)),
                                reads=[wqkv_b, yn_b[ti]], writes=[bankb[bkk]], inc=(kc == 7))
                        S.op("dve", lambda c0=c0, n=n, bq=bq: nc.vector.scalar_tensor_tensor(
                            out=qd[:, c0:c0 + n], in0=banks[bq][:, 0:n], scalar=128.0 ** -0.5, in1=Eb[:, c0:c0 + n],
                            op0=ALU.mult, op1=ALU.mult),
                            reads=[bankb[bq]] + [Eb_b[u] for u in tiles_in], writes=[qd_b[ti]])
                        S.op("dve", lambda c0=c0, n=n, bkk=bkk: nc.vector.tensor_tensor(
                            out=kd[:, c0:c0 + n], in0=Enb[:, c0:c0 + n], in1=banks[bkk][:, 0:n], op=ALU.mult),
                            reads=[bankb[bkk]] + [Enb_b[u] for u in tiles_in], writes=[kd_b[ti]])
                    for i, (c0, n) in enumerate(TOKT if gla_steps >= 2.5 else []):
                        bk = pcount % 8
                        pcount += 1
                        for kc in range(8):
                            S.op("pe", lambda kc=kc, c0=c0, n=n, bk=bk: nc.tensor.matmul(
                                banks[bk][0:n, 0:384], yn[:, kc, c0:c0 + n], wqkv_t[:, kc, 128:512],
                                start=(kc == 0), stop=(kc == 7)),
                                reads=[wqkv_b, yn_b[tt_of_tok(i)]], writes=[bankb[bk]], inc=(kc == 7))
                        S.op("dve", lambda i=i, n=n, bk=bk: nc.vector.tensor_tensor(
                            out=kst[0:n, i, :], in0=Eaft[0:n, i, :], in1=banks[bk][0:n, 0:128], op=ALU.mult),
                            reads=[bankb[bk], Eaft_b[i]], writes=[kst_b[i]])
                        S.op("act", lambda i=i, n=n, bk=bk: nc.scalar.copy(out=vt[0:n, i, :], in_=banks[bk][0:n, 128:384]),
                             reads=[bankb[bk], kst_b[i]], writes=[vt_b[i]])
                    if gla_steps < 3:
                        continue
                    if hd + 1 < 4:
                        load_qkv(hd + 1)
                    S.op("dve", lambda: nc.vector.memset(S32[:], 0.0), writes=[S32_b])
                    order = [16] + list(range(16))

                    def c_att(oi):
                        i = order[oi]
                        c0, n = TOKT[i]
                        ti = tt_of_tok(i)
                        p = oi % 2
                        ba = 0 + p
                        S.op("pe", lambda: nc.tensor.matmul(
                            banks[ba][0:n, 0:n], kd[:, c0:c0 + n], qd[:, c0:c0 + n], start=True, stop=True),
                            reads=[kd_b[ti], qd_b[ti]], writes=[bankb[ba]])
                        S.op("dve", lambda: nc.vector.tensor_tensor(
                            out=attm[p][0:n, 0:n], in0=mask01[0:n, 0:n], in1=banks[ba][0:n, 0:n], op=ALU.mult),
                            reads=[bankb[ba], constb], writes=[att_b[p]])

                    def c_kv(oi):
                        i = order[oi]
                        c0, n = TOKT[i]
                        bs = 4 + (oi % 4)
                        S.op("pe", lambda: nc.tensor.matmul(
                            banks[bs][:, 0:256], kst[0:n, i, :], vt[0:n, i, :], start=True, stop=True),
                            reads=[kst_b[i], vt_b[i]], writes=[bankb[bs]])

                    c_att(0)
                    c_kv(0)
                    for oi, i in enumerate(order):
                        c0, n = TOKT[i]
                        ti = tt_of_tok(i)
                        p = oi % 2
                        bo, bs = 2 + p, 4 + (oi % 4)
                        Sprev, Sprev_b = Sbf2[(oi + 1) % 2], Sbf2_b[(oi + 1) % 2]
                        Snew, Snew_b = Sbf2[oi % 2], Sbf2_b[oi % 2]
                        if oi + 1 < len(order):
                            c_att(oi + 1)
                            c_kv(oi + 1)
                        for c in range(2):
                            S.op("pe", lambda c=c, i=i, p=p, n=n, bo=bo: nc.tensor.matmul(
                                banks[bo][:, c * 128:c * 128 + n], vt[0:n, i, c * 128:(c + 1) * 128], attm[p][0:n, 0:n],
                                start=True, stop=(oi == 0)),
                                reads=[vt_b[i], att_b[p]], writes=[bankb[bo]], inc=(oi == 0 and c == 1))
                            if oi > 0:
                                S.op("pe", lambda c=c, c0=c0, n=n, bo=bo: nc.tensor.matmul(
                                    banks[bo][:, c * 128:c * 128 + n], Sprev[:, c * 128:(c + 1) * 128], qd[:, c0:c0 + n],
                                    start=False, stop=True),
                                    reads=[Sprev_b, qd_b[ti]], writes=[bankb[bo]], inc=(c == 1))
                        src_ap = banks[bo][:, 0:256].rearrange("p (c q) -> p c q", c=2)[:, :, 0:n]
                        S.op("act", lambda src_ap=src_ap, c0=c0, n=n: nc.scalar.copy(
                            out=oTt[:, :, c0:c0 + n], in_=src_ap),
                            reads=[bankb[bo]], writes=[oT_b[i]])
                        S.op("dve", lambda c0=c0, n=n, bs=bs: nc.vector.scalar_tensor_tensor(
                            out=Snew[:], in0=S32[:], scalar=Eb[:, c0 + n - 1:c0 + n], in1=banks[bs][:, 0:256],
                            op0=ALU.mult, op1=ALU.add),
                            reads=[S32_b, Eb_b[i], bankb[bs]], writes=[Snew_b])
                        S.op("dve", lambda c0=c0, n=n, bs=bs: nc.vector.scalar_tensor_tensor(
                            out=S32[:], in0=S32[:], scalar=Eb[:, c0 + n - 1:c0 + n], in1=banks[bs][:, 0:256],
                            op0=ALU.mult, op1=ALU.add),
                            reads=[S32_b, Eb_b[i], bankb[bs]], writes=[S32_b])
                    if gla_steps < 4:
                        continue
                    for ti, (c0, n) in enumerate(TT):
                        tiles_in = [16] if ti == 4 else [4 * ti + u for u in range(4)]
                        p = ti % 2
                        bk = 6 + p
                        S.op("act", lambda p=p, c0=c0, n=n: nc.scalar.activation(
                            out=sq[p][:, :, 0:n], in_=oTt[:, :, c0:c0 + n], func=AF.Square),
                            reads=[oT_b[u] for u in tiles_in], writes=[sq_b[p]])
                        for c in range(2):
                            S.op("pe", lambda c=c, p=p, n=n, bk=bk: nc.tensor.matmul(
                                banks[bk][:, 0:n], ones256, sq[p][:, c, 0:n], start=(c == 0), stop=(c == 1)),
                                reads=[sq_b[p], constb], writes=[bankb[bk]], inc=(c == 1))
                        S.op("act", lambda p=p, n=n, bk=bk: nc.scalar.activation(
                            out=tmp[p][:, 0:n], in_=banks[bk][:, 0:n], func=AF.Ln, bias=epsT[:, 0:1], scale=1.0),
                            reads=[bankb[bk], constb], writes=[tmp_b[p]])
                        S.op("act", lambda p=p, n=n, c0=c0: nc.scalar.activation(
                            out=rstd[:, c0:c0 + n], in_=tmp[p][:, 0:n], func=AF.Exp, scale=-0.5),
                            reads=[tmp_b[p]], writes=[rstd_b[ti]] + [Enb_b[u] for u in tiles_in])
                    def d_rproj(ti):
                        c0, n = TT[ti]
                        for c in range(2):
                            bk = (2 * ti + c) % 4
                            for kc in range(8):
                                S.op("pe", lambda kc=kc: nc.tensor.matmul(
                                    banks[bk][:, 0:n], wr_t[:, kc, c * 128:(c + 1) * 128],
                                    yn[:, kc, c0:c0 + n], start=(kc == 0), stop=(kc == 7)),
                                    reads=[wr_b, yn_b[ti]], writes=[bankb[bk]], inc=(kc == 7))

                    def d_rest(ti):
                        c0, n = TT[ti]
                        tiles_in = [16] if ti == 4 else [4 * ti + u for u in range(4)]
                        p = ti % 2
                        for c in range(2):
                            bk = (2 * ti + c) % 4
                            S.op("act", lambda c=c, bk=bk: nc.scalar.activation(
                                out=sr[p][:, c, 0:n], in_=banks[bk][:, 0:n], func=AF.Silu),
                                reads=[bankb[bk]], writes=[sr_b[p]])
                        for c in range(2):
                            S.op("dve", lambda c=c: nc.vector.scalar_tensor_tensor(
                                out=t1[p][:, c, 0:n], in0=oTt[:, c, c0:c0 + n],
                                scalar=gon[:, j * 8 + hd * 2 + c:j * 8 + hd * 2 + c + 1], in1=rstd[:, c0:c0 + n],
                                op0=ALU.mult, op1=ALU.mult),
                                reads=[oT_b[u] for u in tiles_in] + [rstd_b[ti], constb], writes=[t1_b[p]])
                        S.op("pool", lambda: nc.gpsimd.tensor_tensor(
                            out=og[p][:, :, 0:n], in0=t1[p][:, :, 0:n], in1=sr[p][:, :, 0:n], op=ALU.mult),
                            reads=[t1_b[p], sr_b[p]], writes=[og_b[p]])
                        for dc in range(8):
                            bk = 4 + (dc % 4)
                            for c in range(2):
                                S.op("pe", lambda c=c, dc=dc, bk=bk: nc.tensor.matmul(
                                    banks[bk][:, 0:n], wo_t[:, c, dc * 128:(dc + 1) * 128], og[p][:, c, 0:n],
                                    start=(c == 0), stop=(c == 1)),
                                    reads=[wo_b, og_b[p]], writes=[bankb[bk]], inc=(c == 1))
                            S.op("dve", lambda dc=dc, bk=bk: nc.vector.tensor_tensor(
                                out=hT[:, dc, c0:c0 + n], in0=hT[:, dc, c0:c0 + n], in1=banks[bk][:, 0:n], op=ALU.add),
                                reads=[bankb[bk], h_b[dc][ti]], writes=[h_b[dc][ti]])

                    d_rproj(0)
                    for ti in range(len(TT)):
                        if ti + 1 < len(TT):
                            d_rproj(ti + 1)
                        d_rest(ti)
                    if hd + 1 < 4:
                        load_ro(hd + 1)
                S.barrier()

        sub = 0
        if dbg == "gla":
            gla_mixer(0, 4)
            n_sub = 0
        for l in range(DEPTH):
            for part in range(3):
                if sub >= n_sub:
                    break
                if part == 0:
                    ffn(2 * l, 3 * l)
                elif part == 1:
                    if l % 2 == 0:
                        sb_mixer(l // 2, 3 * l + 1)
                    else:
                        gla_mixer(l // 2, 3 * l + 1)
                else:
                    ffn(2 * l + 1, 3 * l + 2)
                sub += 1

        with ExitStack() as ph:
            ys = [sb("ys%d" % i, [128, D], F32, ph) for i in range(2)]
            ys_b = [Buf(), Buf()]
            for i in range(16):
                s = i % 2
                c0 = i * 128
                for half in range(2):
                    bk = (2 * i + half) % 4
                    for q in range(4):
                        kc = half * 4 + q
                        S.op("pe", lambda bk=bk, q=q, kc=kc, c0=c0: nc.tensor.transpose(
                            out=banks[bk][:, q * 128:(q + 1) * 128], in_=hT[:, kc, c0:c0 + 128], identity=ident),
                            reads=[h_b[kc][i // 4], constb], writes=[bankb[bk]], inc=(q == 3))
                    if half == 0:
                        S.op("act", lambda bk=bk, s=s, half=half: nc.scalar.copy(
                            out=ys[s][:, half * 512:(half + 1) * 512], in_=banks[bk][:, :]),
                            reads=[bankb[bk]], writes=[ys_b[s]])
                    else:
                        S.op("dve", lambda bk=bk, s=s, half=half: nc.vector.tensor_copy(
                            out=ys[s][:, half * 512:(half + 1) * 512], in_=banks[bk][:, :]),
                            reads=[bankb[bk]], writes=[ys_b[s]])
                S.dma("sp", y_d[c0:c0 + 128, :], ys[s][:], d_out[s], reads=[ys_b[s]])
            S.barrier()
    return nc


def _consts():
    i = np.arange(128)
    cf = np.zeros((128, 4, 128), np.float32)
    cf[:, 0] = np.eye(128, dtype=np.float32)
    cf[:, 1] = -(1.0 / 16.0) * (i[:, None] <= i[None, :])
    cf[:, 2] = -(1.0 / 16.0) * (i[:, None] > i[None, :])
    cf[:, 3] = (i[:, None] <= i[None, :])
    cb = np.zeros((128, 9, 128), np.float32)
    cb[:, 0] = 1.0 / 1024.0
    cb[:, 1] = (i[:, None] // 64 == i[None, :] // 64) * (1.0 / 64.0)
    cb[:, 2] = 1.0 / 256.0
    cb[:, 3] = -1.0 * (i[:, None] >= i[None, :])
    cb[:, 4] = -1.0
    cb[:, 5] = np.eye(128, dtype=np.float32)
    cb[:, 6] = np.where(i[:, None] < i[None, :], 0.0, NEG)
    cb[:, 7] = cf[:, 1]
    cb[:, 8] = cf[:, 2]
    return cf.reshape(128, 512), cb.reshape(128, 1152)


def _layout(inp):
    f = lambda a: np.ascontiguousarray(np.asarray(a, dtype=np.float32))
    out = {}
    wgu = np.empty((2 * DEPTH, NFC, 128, 8, 256), np.float32)
    wd = np.empty((2 * DEPTH, 2, 8, 128, 11, 128), np.float32)
    for l in range(DEPTH):
        for wh, (kgu, kd) in enumerate((("ffn_a_w_gu", "ffn_a_w_down"), ("ffn_b_w_gu", "ffn_b_w_down"))):
            w = f(inp[kgu][l]).reshape(8, 128, 2, NFC, 128)
            wgu[2 * l + wh] = w.transpose(3, 1, 0, 2, 4).reshape(NFC, 128, 8, 256)
            w2 = f(inp[kd][l]).reshape(2, 11, 128, 8, 128)
            wd[2 * l + wh] = w2.transpose(0, 3, 2, 1, 4)
    out["wgu"] = wgu.reshape(2 * DEPTH, NFC, 128, 8 * 256)
    out["wd"] = wd.reshape(2 * DEPTH, 2, 8, 128, 11 * 128)
    wqkv = f(inp["sb_w_qkv"]).reshape(2, 8, 128, 3, 4, 256)
    out["wqkv"] = np.ascontiguousarray(wqkv.transpose(0, 4, 2, 1, 3, 5)).reshape(2, 4, 128, 8 * 768)
    wosb = f(inp["sb_w_o"]).reshape(2, 4, 2, 128, 1024)
    out["wosb"] = np.ascontiguousarray(wosb.transpose(0, 1, 3, 2, 4)).reshape(2, 4, 128, 2048)
    w_in = f(inp["gla_w_in"])
    win = np.empty((2, 4, 128, 8, 768), np.float32)
    for hd in range(4):
        cols = np.concatenate([np.arange(hd * 128, hd * 128 + 128), 512 + np.arange(hd * 128, hd * 128 + 128),
                               1024 + np.arange(hd * 256, hd * 256 + 256), 2048 + np.arange(hd * 256, hd * 256 + 256)])
        win[:, hd] = w_in[:, :, cols].reshape(2, 8, 128, 768).transpose(0, 2, 1, 3)
    out["win"] = np.ascontiguousarray(win[:, :, :, :, 0:512]).reshape(2, 4, 128, 8 * 512)
    out["winr"] = np.ascontiguousarray(win[:, :, :, :, 512:768]).reshape(2, 4, 128, 8 * 256)
    out["wgl"] = np.ascontiguousarray(w_in[:, :, 3072:3088].reshape(2, 8, 128, 16).transpose(0, 2, 1, 3)).reshape(2, 128, 128)
    out["wga"] = np.ascontiguousarray(np.concatenate([f(inp["gla_w_gate_up"]), f(inp["gla_b_gate"])[:, None, :]], axis=1))
    wogl = f(inp["gla_w_o"]).reshape(2, 4, 2, 128, 1024)
    out["wogl"] = np.ascontiguousarray(wogl.transpose(0, 1, 3, 2, 4)).reshape(2, 4, 128, 2048)
    norms = np.stack([f(inp["ffn_a_norm"]), f(inp["mix_norm"]), f(inp["ffn_b_norm"])], axis=1)
    out["norms"] = np.ascontiguousarray(norms.reshape(12, 8, 128).transpose(2, 0, 1)).reshape(128, 96)
    out["gon"] = np.ascontiguousarray(f(inp["gla_out_norm"]).reshape(2, 8, 128).transpose(2, 0, 1)).reshape(128, 16)
    gq = f(inp["sb_q_norm"])
    gk = f(inp["sb_k_norm"])
    gqk = np.stack([gq[0], gk[0], gq[1], gk[1]], axis=1)
    out["gqk"] = np.ascontiguousarray(np.concatenate([gqk, gqk], axis=0))
    out["meta"] = f(inp["meta"])
    cf, cb = _consts()
    out["cf"] = cf
    out["cb"] = cb
    return out


_NC_CACHE = {}


def kernel(**inputs):
    shared = _layout(inputs)
    x = np.ascontiguousarray(np.asarray(inputs["x"], dtype=np.float32))
    if "nc" not in _NC_CACHE:
        _NC_CACHE["nc"] = build_program()
    nc = _NC_CACHE["nc"]
    in_maps = []
    for b in range(8):
        m = dict(shared)
        m["x"] = x[b]
        in_maps.append(m)
    res = run_bass_kernel_spmd(nc, in_maps, core_ids=list(range(8)))
    return np.stack([np.asarray(r["y"], dtype=np.float32) for r in res.results], axis=0)
```

```python
from contextlib import ExitStack

import numpy as np
import concourse.bass as bass
import concourse.mybir as mybir
from concourse.bass_utils import run_bass_kernel_spmd

F32 = mybir.dt.float32
BF16 = mybir.dt.bfloat16
AF = mybir.ActivationFunctionType
ALU = mybir.AluOpType

D = 1024
SEQ = 2048
NMETA = 16
T = SEQ + NMETA
DFF = 2816
NFC = 22
DEPTH = 4
EPS = 1e-6
TT = [(0, 512), (512, 512), (1024, 512), (1536, 512), (2048, 16)]
TOKT = [(i * 128, 128) for i in range(16)] + [(2048, 16)]
NEG = -30000.0


class Buf:
    __slots__ = ("w", "r")

    def __init__(self):
        self.w = None
        self.r = {}


class Sched:
    def __init__(self, nc, st):
        self.nc = nc
        self.st = st
        self.E = {"pe": nc.tensor, "act": nc.scalar, "dve": nc.vector, "pool": nc.gpsimd, "sp": nc.sync}
        self.sems = {}
        self.cnt = {}
        for k in ("pe", "act", "dve", "pool"):
            self.sems[k] = st.enter_context(nc.semaphore("s_" + k))
            self.cnt[k] = 0
        self.seen = {e: {} for e in self.E}
        self.ndma = 0

    def dma_sem(self):
        k = "d%d" % self.ndma
        self.ndma += 1
        self.sems[k] = self.st.enter_context(self.nc.semaphore("s_" + k))
        self.cnt[k] = 0
        return k

    def wait(self, eng, toks):
        need = {}
        for t in toks:
            if t is None:
                continue
            k, v = t
            if eng == "pe" and k == "pe":
                continue
            if self.seen[eng].get(k, 0) >= v:
                continue
            if need.get(k, 0) < v:
                need[k] = v
        for k, v in need.items():
            self.E[eng].wait_ge(self.sems[k], v)
            self.seen[eng][k] = v

    @staticmethod
    def deps(reads, writes):
        toks = []
        for b in reads:
            toks.append(b.w)
        for b in writes:
            toks.append(b.w)
            toks.extend(b.r.items())
        return toks

    def _mark(self, tok, reads, writes):
        k, v = tok
        for b in reads:
            if b.r.get(k, 0) < v:
                b.r[k] = v
        for b in writes:
            b.w = tok
            b.r = {}

    def op(self, eng, fn, reads=(), writes=(), inc=True):
        self.wait(eng, self.deps(reads, writes))
        ins = fn()
        if inc:
            ins.then_inc(self.sems[eng], 1)
            self.cnt[eng] += 1
            tok = (eng, self.cnt[eng])
        else:
            tok = (eng, self.cnt[eng] + 1)
        self._mark(tok, reads, writes)
        return tok

    def dma(self, eng, out, in_, semk, reads=(), writes=()):
        self.wait(eng, self.deps(reads, writes))
        self.E[eng].dma_start(out=out, in_=in_).then_inc(self.sems[semk], 16)
        self.cnt[semk] += 16
        tok = (semk, self.cnt[semk])
        self._mark(tok, reads, writes)
        return tok

    def barrier(self):
        toks = [(k, v) for k, v in self.cnt.items() if v > 0]
        for e in self.E:
            self.wait(e, toks)


def build_program(n_sub=3 * DEPTH, dbg=False, gla_steps=4):
    nc = bass.Bass("TRN2", target_bir_lowering=False)
    dt = nc.dram_tensor
    x_d = dt("x", [SEQ, D], F32, kind="ExternalInput").ap()
    meta_d = dt("meta", [NMETA, D], F32, kind="ExternalInput").ap()
    wgu_d = dt("wgu", [2 * DEPTH, NFC, 128, 8 * 256], F32, kind="ExternalInput").ap()
    wd_d = dt("wd", [2 * DEPTH, 2, 8, 128, 11 * 128], F32, kind="ExternalInput").ap()
    wqkv_d = dt("wqkv", [2, 4, 128, 8 * 768], F32, kind="ExternalInput").ap()
    wosb_d = dt("wosb", [2, 4, 128, 2 * 1024], F32, kind="ExternalInput").ap()
    win_d = dt("win", [2, 4, 128, 8 * 512], F32, kind="ExternalInput").ap()
    winr_d = dt("winr", [2, 4, 128, 8 * 256], F32, kind="ExternalInput").ap()
    wgl_d = dt("wgl", [2, 128, 8 * 16], F32, kind="ExternalInput").ap()
    wga_d = dt("wga", [2, 17, 512], F32, kind="ExternalInput").ap()
    wogl_d = dt("wogl", [2, 4, 128, 2 * 1024], F32, kind="ExternalInput").ap()
    norms_d = dt("norms", [128, 12 * 8], F32, kind="ExternalInput").ap()
    gon_d = dt("gon", [128, 2 * 8], F32, kind="ExternalInput").ap()
    gqk_d = dt("gqk", [128, 4], F32, kind="ExternalInput").ap()
    cf_d = dt("cf", [128, 4 * 128], F32, kind="ExternalInput").ap()
    cb_d = dt("cb", [128, 9 * 128], F32, kind="ExternalInput").ap()
    y_d = dt("y", [SEQ, D], F32, kind="ExternalOutput").ap()
    if dbg:
        dq_d = dt("dq", [128, 2, T], BF16, kind="ExternalOutput").ap()
        dk_d = dt("dk", [128, 2, T], BF16, kind="ExternalOutput").ap()
        dv_d = dt("dv", [128, 17, 256], BF16, kind="ExternalOutput").ap()
        do_d = dt("do", [128, 2, T], BF16, kind="ExternalOutput").ap()

    with ExitStack() as st:
        S = Sched(nc, st)
        uniq = [0]

        def sb(name, shape, dtype, ctx=st):
            uniq[0] += 1
            return ctx.enter_context(nc.sbuf_tensor("%s_%d" % (name, uniq[0]), shape, dtype))
        hT = sb("hT", [128, 8, T], F32)
        yn = sb("yn", [128, 8, T], BF16)
        norms = sb("norms_s", [128, 96], F32)
        gon = sb("gon_s", [128, 16], F32)
        gqk = sb("gqk_s", [128, 4], F32)
        cf = sb("cf_s", [128, 4 * 128], F32)
        cb = sb("cb_s", [128, 9 * 128], BF16)
        epsT = sb("eps_s", [128, 1], F32)
        banks = [st.enter_context(nc.psum_tensor("bank%d" % i, [128, 512], F32)) for i in range(8)]
        bankb = [Buf() for _ in range(8)]
        ident = cf[:, 0:128]
        TIs = cb[:, 896:1024]
        SUs = cb[:, 1024:1152]
        mask01 = cf[:, 384:512]
        ones_mean = cb[:, 0:128]
        blk64 = cb[:, 128:256]
        ones256 = cb[:, 256:384]
        negtri = cb[:, 384:512]
        negones = cb[:, 512:640]
        identb = cb[:, 640:768]
        DM = cb[:, 768:896]
        constb = Buf()
        constb2 = Buf()
        h_b = [[Buf() for _ in range(5)] for _ in range(8)]
        yn_b = [Buf() for _ in range(5)]
        d_const = S.dma_sem()
        d_const2 = S.dma_sem()
        d_in = [S.dma_sem(), S.dma_sem()]
        d_out = [S.dma_sem(), S.dma_sem()]

        S.dma("sp", norms[:], norms_d, d_const, writes=[constb])
        S.dma("sp", gon[:], gon_d, d_const, writes=[constb])
        S.dma("sp", gqk[:], gqk_d, d_const, writes=[constb])
        S.dma("sp", cf[:], cf_d, d_const, writes=[constb])
        S.dma("pool", cb[:], cb_d, d_const2, writes=[constb2])
        S.op("dve", lambda: nc.vector.memset(epsT[:], EPS), writes=[constb])
        S.op("dve", lambda: nc.vector.tensor_scalar_mul(out=gqk[:, 0:1], in0=gqk[:, 0:1], scalar1=0.125),
             reads=[constb], writes=[constb])
        S.op("dve", lambda: nc.vector.tensor_scalar_mul(out=gqk[:, 2:3], in0=gqk[:, 2:3], scalar1=0.125),
             reads=[constb], writes=[constb])
        S.barrier()

        def tt_of_tok(i):
            return 4 if i == 16 else i // 4

        with ExitStack() as ph:
            xs = [sb("xs%d" % i, [128, D], F32, ph) for i in range(2)]
            xs_b = [Buf(), Buf()]
            for i, (c0, n) in enumerate(TOKT):
                s = i % 2
                src = x_d[c0:c0 + n, :] if i < 16 else meta_d
                S.dma("sp", xs[s][0:n, :], src, d_in[s], writes=[xs_b[s]])
                for half in range(2):
                    bk = (2 * i + half) % 4
                    pb = banks[bk]
                    for q in range(4):
                        kc = half * 4 + q
                        S.op("pe", lambda pb=pb, q=q, kc=kc, s=s, n=n: nc.tensor.transpose(
                            out=pb[:, q * 128:q * 128 + n], in_=xs[s][0:n, kc * 128:(kc + 1) * 128],
                            identity=ident[0:n, 0:n]),
                            reads=[xs_b[s], constb], writes=[bankb[bk]], inc=(q == 3))
                    src_ap = pb[:].rearrange("p (q c) -> p q c", q=4)[:, :, 0:n]
                    wr = [h_b[half * 4 + q][tt_of_tok(i)] for q in range(4)]
                    eng = "act" if half == 0 else "dve"
                    if eng == "act":
                        S.op("act", lambda src_ap=src_ap, half=half, c0=c0, n=n: nc.scalar.copy(
                            out=hT[:, half * 4:half * 4 + 4, c0:c0 + n], in_=src_ap),
                            reads=[bankb[bk]], writes=wr)
                    else:
                        S.op("dve", lambda src_ap=src_ap, half=half, c0=c0, n=n: nc.vector.tensor_copy(
                            out=hT[:, half * 4:half * 4 + 4, c0:c0 + n], in_=src_ap),
                            reads=[bankb[bk]], writes=wr)
            S.barrier()

        def rmsnorm(gidx, sq, sq_b, tmp, tmp_b, rstd, rstd_b):
            r = 0
            for ti, (c0, n) in enumerate(TT):
                bk = 6 + (ti % 2)
                for kc in range(8):
                    q = r % len(sq)
                    r += 1
                    S.op("act", lambda kc=kc, q=q, c0=c0, n=n: nc.scalar.activation(
                        out=sq[q][:, 0:n], in_=hT[:, kc, c0:c0 + n], func=AF.Square),
                        reads=[h_b[kc][ti]], writes=[sq_b[q]])
                    S.op("pe", lambda kc=kc, q=q, n=n, bk=bk: nc.tensor.matmul(
                        banks[bk][:, 0:n], ones_mean, sq[q][:, 0:n], start=(kc == 0), stop=(kc == 7)),
                        reads=[sq_b[q], constb2], writes=[bankb[bk]])
                pt = ti % len(tmp)
                pr = ti % len(rstd)
                S.op("act", lambda pt=pt, n=n, bk=bk: nc.scalar.activation(
                    out=tmp[pt][:, 0:n], in_=banks[bk][:, 0:n], func=AF.Ln, bias=epsT[:, 0:1], scale=1.0),
                    reads=[bankb[bk], constb], writes=[tmp_b[pt]])
                S.op("act", lambda pt=pt, pr=pr, n=n: nc.scalar.activation(
                    out=rstd[pr][:, 0:n], in_=tmp[pt][:, 0:n], func=AF.Exp, scale=-0.5),
                    reads=[tmp_b[pt]], writes=[rstd_b[pr]])
                for kc in range(8):
                    S.op("dve", lambda kc=kc, pr=pr, c0=c0, n=n: nc.vector.scalar_tensor_tensor(
                        out=yn[:, kc, c0:c0 + n], in0=hT[:, kc, c0:c0 + n],
                        scalar=norms[:, gidx * 8 + kc:gidx * 8 + kc + 1], in1=rstd[pr][:, 0:n],
                        op0=ALU.mult, op1=ALU.mult),
                        reads=[h_b[kc][ti], rstd_b[pr], constb], writes=[yn_b[ti]])

        def ffn(fidx, gidx):
            with ExitStack() as ph:
                n_sq = [sb("f_nsq%d" % i, [128, 512], BF16, ph) for i in range(4)]
                n_tmp = [sb("f_ntmp%d" % i, [128, 512], F32, ph) for i in range(2)]
                n_rs = [sb("f_nrs%d" % i, [128, 512], F32, ph) for i in range(2)]
                act = sb("f_act", [128, 11, T], BF16, ph)
                NGS = 5
                wgu = [sb("f_wgu%d" % i, [128, 8, 256], BF16, ph) for i in range(NGS)]
                wdn = [sb("f_wd%d" % i, [128, 11, 128], BF16, ph) for i in range(3)]
                sg = [sb("f_sg%d" % i, [128, 512], F32, ph) for i in range(2)]
                act_b = [[Buf() for _ in range(5)] for _ in range(11)]
                wgu_b = [Buf() for _ in range(NGS)]
                wdn_b = [Buf() for _ in range(3)]
                sg_b = [Buf(), Buf()]
                wgu_s = [S.dma_sem() for _ in range(NGS)]
                wdn_s = [S.dma_sem() for _ in range(3)]
                steps = []
                for grp in range(2):
                    for fci in range(11):
                        steps.append(("gu", grp, fci))
                    for dc in range(8):
                        steps.append(("d", grp, dc))
                kcount = {"gu": 0, "d": 0}
                slot_of = []
                for s_ in steps:
                    slot_of.append(kcount[s_[0]] % (NGS if s_[0] == "gu" else 3))
                    kcount[s_[0]] += 1
                loaded = [0]

                def load(m):
                    kind, grp, idx = steps[m]
                    s = slot_of[m]
                    if kind == "gu":
                        fc = grp * 11 + idx
                        S.dma("pool", wgu[s][:].rearrange("p a b -> p (a b)"), wgu_d[fidx, fc],
                              wgu_s[s], writes=[wgu_b[s]])
                    else:
                        S.dma("pool", wdn[s][:].rearrange("p a b -> p (a b)"), wd_d[fidx, grp, idx],
                              wdn_s[s], writes=[wdn_b[s]])

                for m0 in range(NGS):
                    load(m0)
                loaded[0] = NGS
                rmsnorm(gidx, n_sq, [Buf() for _ in n_sq], n_tmp, [Buf() for _ in n_tmp], n_rs, [Buf() for _ in n_rs])
                gstate = [0]

                def gu_tile(m, ti):
                    kind, grp, fci = steps[m]
                    s = slot_of[m]
                    c0, n = TT[ti]
                    p = gstate[0] % 2
                    gstate[0] += 1
                    bg, bu = p, 2 + p
                    for kc in range(8):
                        S.op("pe", lambda kc=kc: nc.tensor.matmul(
                            banks[bg][:, 0:n], wgu[s][:, kc, 0:128], yn[:, kc, c0:c0 + n],
                            start=(kc == 0), stop=(kc == 7)),
                            reads=[wgu_b[s], yn_b[ti]], writes=[bankb[bg]], inc=(kc == 7))
                    for kc in range(8):
                        S.op("pe", lambda kc=kc: nc.tensor.matmul(
                            banks[bu][:, 0:n], wgu[s][:, kc, 128:256], yn[:, kc, c0:c0 + n],
                            start=(kc == 0), stop=(kc == 7)),
                            reads=[wgu_b[s], yn_b[ti]], writes=[bankb[bu]], inc=(kc == 7))
                    S.op("act", lambda: nc.scalar.activation(
                        out=sg[p][:, 0:n], in_=banks[bg][:, 0:n], func=AF.Silu),
                        reads=[bankb[bg]], writes=[sg_b[p]])
                    S.op("dve", lambda: nc.vector.tensor_tensor(
                        out=act[:, fci, c0:c0 + n], in0=sg[p][:, 0:n], in1=banks[bu][:, 0:n], op=ALU.mult),
                        reads=[sg_b[p], bankb[bu]], writes=[act_b[fci][ti]])

                for ti in range(len(TT)):
                    for m in range(3):
                        gu_tile(m, ti)
                gcount = 0
                for m, (kind, grp, idx) in enumerate(steps):
                    if m < 3:
                        continue
                    while loaded[0] < len(steps) and loaded[0] <= m + 2:
                        load(loaded[0])
                        loaded[0] += 1
                    s = slot_of[m]
                    if kind == "gu":
                        for ti in range(len(TT)):
                            gu_tile(m, ti)
                    else:
                        dc = idx
                        for ti, (c0, n) in enumerate(TT):
                            p = gcount % 2
                            gcount += 1
                            bo = 4 + p
                            for fci in range(11):
                                S.op("pe", lambda fci=fci, s=s, c0=c0, n=n, bo=bo: nc.tensor.matmul(
                                    banks[bo][:, 0:n], wdn[s][:, fci, :], act[:, fci, c0:c0 + n],
                                    start=(fci == 0), stop=(fci == 10)),
                                    reads=[wdn_b[s], act_b[fci][ti]], writes=[bankb[bo]], inc=(fci == 10))
                            S.op("dve", lambda dc=dc, c0=c0, n=n, bo=bo: nc.vector.scalar_tensor_tensor(
                                out=hT[:, dc, c0:c0 + n], in0=banks[bo][:, 0:n], scalar=0.5,
                                in1=hT[:, dc, c0:c0 + n], op0=ALU.mult, op1=ALU.add),
                                reads=[bankb[bo], h_b[dc][ti]], writes=[h_b[dc][ti]])
                S.barrier()

        def rstd_from(bk, np_, n, tmp_t, tmp_b, out_t, out_b):
            S.op("act", lambda: nc.scalar.activation(
                out=tmp_t[0:np_, 0:n], in_=banks[bk][0:np_, 0:n], func=AF.Ln, bias=epsT[0:np_, 0:1], scale=1.0),
                reads=[bankb[bk], constb], writes=[tmp_b])
            S.op("act", lambda: nc.scalar.activation(
                out=out_t[0:np_, 0:n], in_=tmp_t[0:np_, 0:n], func=AF.Exp, scale=-0.5),
                reads=[tmp_b], writes=[out_b])

        def sb_mixer(j, gidx):
            with ExitStack() as ph:
                wq = [sb("s_wq%d" % i, [128, 8, 768], BF16, ph) for i in range(2)]
                wo = [sb("s_wo%d" % i, [128, 2, 1024], BF16, ph) for i in range(2)]
                qT = sb("s_qT", [128, 2, T], BF16, ph)
                kT = sb("s_kT", [128, 2, T], BF16, ph)
                vg = sb("s_v", [128, 17, 256], BF16, ph)
                oT = sb("s_oT", [128, 2, T], BF16, ph)
                sq = [sb("s_sq%d" % i, [128, 512], BF16, ph) for i in range(2)]
                tmp = [sb("s_tmp%d" % i, [128, 512], F32, ph) for i in range(2)]
                rs = [sb("s_rs%d" % i, [128, 512], F32, ph) for i in range(2)]
                et = [sb("s_e%d" % i, [128, 512], F32, ph) for i in range(4)]
                spt = [sb("s_sp%d" % i, [128, 512], BF16, ph) for i in range(4)]
                wt = [sb("s_w%d" % i, [128, 512], BF16, ph) for i in range(4)]
                A32 = [sb("s_A32_%d" % i, [128, 512], F32, ph) for i in range(2)]
                Abf = [sb("s_Abf_%d" % i, [128, 512], BF16, ph) for i in range(2)]
                wq_b = [Buf(), Buf()]
                wo_b = [Buf(), Buf()]
                wq_s = [S.dma_sem(), S.dma_sem()]
                wo_s = [S.dma_sem(), S.dma_sem()]
                qT_b = [[Buf() for _ in range(5)] for _ in range(2)]
                kT_b = [[Buf() for _ in range(5)] for _ in range(2)]
                v_b = [Buf() for _ in range(17)]
                oT_b = [[Buf() for _ in range(5)] for _ in range(2)]
                sq_b = [Buf(), Buf()]
                tmp_b = [Buf(), Buf()]
                rs_b = [Buf(), Buf()]
                et_b = [Buf() for _ in range(4)]
                sp_b = [Buf() for _ in range(4)]
                w_b = [Buf() for _ in range(4)]
                A32_b = [Buf(), Buf()]
                Abf_b = [Buf(), Buf()]
                rmsnorm(gidx, sq, sq_b, tmp, tmp_b, rs, rs_b)

                def loadw(hg):
                    s = hg % 2
                    S.dma("pool", wq[s][:].rearrange("p a b -> p (a b)"), wqkv_d[j, hg], wq_s[s], writes=[wq_b[s]])
                    S.dma("pool", wo[s][:].rearrange("p a b -> p (a b)"), wosb_d[j, hg], wo_s[s], writes=[wo_b[s]])

                loadw(0)
                pcount = 0
                for hg in range(4):
                    if hg + 1 < 4:
                        loadw(hg + 1)
                    ws = hg % 2
                    ptiles = []
                    for which, dst, dst_b, gcol in ((0, qT, qT_b, 2 * j), (1, kT, kT_b, 2 * j + 1)):
                        for c in range(2):
                            for ti, (c0, n) in enumerate(TT):
                                ptiles.append((which, dst, dst_b, gcol, c, ti, c0, n))

                    def p_stage1(t):
                        which, dst, dst_b, gcol, c, ti, c0, n = ptiles[t]
                        bk = 2 * (t % 4)
                        col = which * 256 + c * 128
                        for kc in range(8):
                            S.op("pe", lambda kc=kc: nc.tensor.matmul(
                                banks[bk][:, 0:n], wq[ws][:, kc, col:col + 128], yn[:, kc, c0:c0 + n],
                                start=(kc == 0), stop=(kc == 7)),
                                reads=[wq_b[ws], yn_b[ti]], writes=[bankb[bk]], inc=(kc == 7))

                    def p_stage2(t):
                        which, dst, dst_b, gcol, c, ti, c0, n = ptiles[t]
                        bk = 2 * (t % 4)
                        bk2 = bk + 1
                        p = t % 2
                        S.op("act", lambda: nc.scalar.activation(
                            out=sq[p][:, 0:n], in_=banks[bk][:, 0:n], func=AF.Square),
                            reads=[bankb[bk]], writes=[sq_b[p]])
                        S.op("pe", lambda: nc.tensor.matmul(
                            banks[bk2][:, 0:n], blk64, sq[p][:, 0:n], start=True, stop=True),
                            reads=[sq_b[p], constb], writes=[bankb[bk2]])
                        rstd_from(bk2, 128, n, tmp[p], tmp_b[p], rs[p], rs_b[p])
                        S.op("dve", lambda: nc.vector.scalar_tensor_tensor(
                            out=dst[:, c, c0:c0 + n], in0=banks[bk][:, 0:n], scalar=gqk[:, gcol:gcol + 1],
                            in1=rs[p][:, 0:n], op0=ALU.mult, op1=ALU.mult),
                            reads=[bankb[bk], rs_b[p], constb], writes=[dst_b[c][ti]])

                    p_stage1(0)
                    for t in range(len(ptiles)):
                        if t + 1 < len(ptiles):
                            p_stage1(t + 1)
                        p_stage2(t)
                    for i, (c0, n) in enumerate(TOKT):
                        bk = pcount % 8
                        pcount += 1
                        for kc in range(8):
                            S.op("pe", lambda kc=kc, c0=c0, n=n, bk=bk: nc.tensor.matmul(
                                banks[bk][0:n, 0:256], yn[:, kc, c0:c0 + n], wq[ws][:, kc, 512:768],
                                start=(kc == 0), stop=(kc == 7)),
                                reads=[wq_b[ws], yn_b[tt_of_tok(i)]], writes=[bankb[bk]], inc=(kc == 7))
                        if i % 2 == 0:
                            S.op("act", lambda i=i, n=n, bk=bk: nc.scalar.copy(
                                out=vg[0:n, i, :], in_=banks[bk][0:n, 0:256]), reads=[bankb[bk]], writes=[v_b[i]])
                        else:
                            S.op("dve", lambda i=i, n=n, bk=bk: nc.vector.tensor_copy(
                                out=vg[0:n, i, :], in_=banks[bk][0:n, 0:256]), reads=[bankb[bk]], writes=[v_b[i]])
                    batches = []
                    for c in range(2):
                        for q4 in range(4):
                            for Sb in range(4 * q4 + 3, -1, -1):
                                batches.append((c, q4, Sb))
                            batches.append((c, q4, 16))
                        batches.append((c, 4, 16))
                    ocount = [0]
                    o_bank_of = {}

                    def geom(b):
                        c, q4, Sb = b
                        if q4 == 4:
                            return dict(c=c, q4=4, Sb=16, nk=16, k0=2048, qc0=2048, lc0=0, n=16, diag=True,
                                        first=True, last=True, useA=False)
                        if Sb == 16:
                            return dict(c=c, q4=q4, Sb=16, nk=16, k0=2048, qc0=512 * q4, lc0=0, n=512, diag=False,
                                        first=False, last=True, useA=True)
                        lc0 = max(0, Sb - 4 * q4) * 128
                        return dict(c=c, q4=q4, Sb=Sb, nk=128, k0=Sb * 128, qc0=512 * q4 + lc0, lc0=lc0, n=512 - lc0,
                                    diag=(Sb >= 4 * q4), first=(Sb == 4 * q4 + 3), last=False,
                                    useA=(Sb != 4 * q4 + 3))

                    def zmm(bi, g, hh):
                        pbase = hh * 64
                        zb = (bi % 3) * 2 + hh
                        nk, n = g["nk"], g["n"]
                        ti_q = g["q4"]
                        ti_k = tt_of_tok(g["Sb"])
                        kap = kT[pbase:pbase + 64, g["c"], g["k0"]:g["k0"] + nk]
                        rd = [kT_b[g["c"]][ti_k], qT_b[g["c"]][ti_q]]
                        S.op("pe", lambda: nc.tensor.matmul(
                            banks[zb][0:nk, 0:n], kap, qT[pbase:pbase + 64, g["c"], g["qc0"]:g["qc0"] + n],
                            start=True, stop=(not g["diag"])), reads=rd, writes=[bankb[zb]], inc=(not g["diag"]))
                        if g["diag"]:
                            nd = min(128, n)
                            S.op("pe", lambda: nc.tensor.matmul(
                                banks[zb][0:nk, 0:nd], identb[0:nk, 0:nk], DM[0:nk, 0:nd], start=False, stop=True),
                                reads=[constb], writes=[bankb[zb]])

                    def el(bi, g, hh):
                        zb = (bi % 3) * 2 + hh
                        ei = (2 * bi + hh) % 4
                        nk, n = g["nk"], g["n"]
                        S.op("act", lambda: nc.scalar.activation(
                            out=et[ei][0:nk, 0:n], in_=banks[zb][0:nk, 0:n], func=AF.Exp),
                            reads=[bankb[zb]], writes=[et_b[ei]])
                        S.op("act", lambda: nc.scalar.activation(
                            out=spt[ei][0:nk, 0:n], in_=et[ei][0:nk, 0:n], func=AF.Ln, bias=1.0, scale=1.0),
                            reads=[et_b[ei]], writes=[sp_b[ei]])

                    def ta(bi, g, hh):
                        zb = (bi % 3) * 2 + hh
                        ei = (2 * bi + hh) % 4
                        nk, n, lc0 = g["nk"], g["n"], g["lc0"]
                        S.op("pe", lambda: nc.tensor.matmul(
                            banks[zb][0:nk, 0:n], negtri[0:nk, 0:nk], spt[ei][0:nk, 0:n], start=False,
                            stop=True, skip_group_check=True), reads=[sp_b[ei], constb, et_b[ei]], writes=[bankb[zb]],
                            inc=(not g["useA"]))
                        if g["useA"]:
                            S.op("pe", lambda: nc.tensor.matmul(
                                banks[zb][0:nk, 0:n], negones[:, 0:nk], Abf[hh][:, lc0:lc0 + n], start=False, stop=True,
                                skip_group_check=True),
                                reads=[Abf_b[hh], constb], writes=[bankb[zb]])

                    def aupd(bi, g, hh):
                        if g["last"]:
                            return
                        zb = (bi % 3) * 2 + hh
                        ei = (2 * bi + hh) % 4
                        n, lc0 = g["n"], g["lc0"]
                        if g["first"]:
                            S.op("pool", lambda: nc.gpsimd.memset(A32[hh][:], 0.0), writes=[A32_b[hh]])
                        S.op("pool", lambda: nc.gpsimd.tensor_tensor(
                            out=A32[hh][:, lc0:lc0 + n], in0=A32[hh][:, lc0:lc0 + n], in1=spt[ei][:, 0:n], op=ALU.add),
                            reads=[sp_b[ei], A32_b[hh]], writes=[A32_b[hh]])
                        if g["first"]:
                            S.op("dve", lambda: nc.vector.tensor_copy(out=Abf[hh][:], in_=A32[hh][:]),
                                 reads=[A32_b[hh]], writes=[Abf_b[hh]])
                        else:
                            S.op("dve", lambda: nc.vector.tensor_copy(out=Abf[hh][:, lc0:lc0 + n],
                                                                      in_=A32[hh][:, lc0:lc0 + n]),
                                 reads=[A32_b[hh]], writes=[Abf_b[hh]])

                    def ea(bi, g, hh):
                        zb = (bi % 3) * 2 + hh
                        ei = (2 * bi + hh) % 4
                        nk, n = g["nk"], g["n"]
                        S.op("act", lambda: nc.scalar.activation(
                            out=wt[ei][0:nk, 0:n], in_=banks[zb][0:nk, 0:n], func=AF.Exp),
                            reads=[bankb[zb]], writes=[w_b[ei]])

                    def pv(bi, g, hh):
                        zb = (bi % 3) * 2 + hh
                        ei = (2 * bi + hh) % 4
                        pbase = hh * 64
                        nk, n, lc0 = g["nk"], g["n"], g["lc0"]
                        key = (g["c"], g["q4"])
                        if key not in o_bank_of:
                            o_bank_of[key] = 6 + (ocount[0] % 2)
                            ocount[0] += 1
                        ob = o_bank_of[key]
                        vi = g["Sb"]
                        vap = vg[0:nk, vi, (2 * g["c"] + hh) * 64:(2 * g["c"] + hh) * 64 + 64]
                        tp = (0, pbase)
                        rd = [v_b[vi], w_b[ei]]
                        S.op("pe", lambda: nc.tensor.matmul(
                            banks[ob][pbase:pbase + 64, lc0:lc0 + n], vap, wt[ei][0:nk, 0:n], start=g["first"],
                            stop=True, tile_position=tp, skip_group_check=True), reads=rd, writes=[bankb[ob]])

                    def oevac(g):
                        key = (g["c"], g["q4"])
                        ob = o_bank_of[key]
                        n = 16 if g["q4"] == 4 else 512
                        c0 = 2048 if g["q4"] == 4 else 512 * g["q4"]
                        S.op("dve", lambda: nc.vector.tensor_copy(out=oT[:, g["c"], c0:c0 + n], in_=banks[ob][:, 0:n]),
                             reads=[bankb[ob]], writes=[oT_b[g["c"]][g["q4"]]])

                    geoms = [geom(b) for b in batches]
                    nb = len(geoms)
                    for b0_ in range(min(2, nb)):
                        for hh in range(2):
                            zmm(b0_, geoms[b0_], hh)
                    for k in range(2 * nb):
                        bi, hh = divmod(k, 2)
                        g = geoms[bi]
                        el(bi, g, hh)
                        if k >= 1:
                            pb_, ph_ = divmod(k - 1, 2)
                            ea(pb_, geoms[pb_], ph_)
                        if hh == 1 and bi + 2 < nb:
                            for h2 in range(2):
                                zmm(bi + 2, geoms[bi + 2], h2)
                        ta(bi, g, hh)
                        aupd(bi, g, hh)
                        if k >= 1:
                            pv(pb_, geoms[pb_], ph_)
                            if ph_ == 1 and geoms[pb_]["last"]:
                                oevac(geoms[pb_])
                    ea(nb - 1, geoms[nb - 1], 1)
                    pv(nb - 1, geoms[nb - 1], 1)
                    oevac(geoms[nb - 1])
                    if dbg and hg == 0:
                        dsem = S.dma_sem()
                        S.dma("sp", dq_d, qT[:], dsem, reads=[b for r in qT_b for b in r])
                        S.dma("sp", dk_d, kT[:], dsem, reads=[b for r in kT_b for b in r])
                        S.dma("sp", dv_d, vg[:], dsem, reads=v_b)
                        S.dma("sp", do_d, oT[:], dsem, reads=[b for r in oT_b for b in r])
                    for dc in range(8):
                        for ti, (c0, n) in enumerate(TT):
                            bk = pcount % 8
                            pcount += 1
                            for c in range(2):
                                S.op("pe", lambda c=c, dc=dc, c0=c0, n=n, bk=bk: nc.tensor.matmul(
                                    banks[bk][:, 0:n], wo[ws][:, c, dc * 128:(dc + 1) * 128], oT[:, c, c0:c0 + n],
                                    start=(c == 0), stop=(c == 1)),
                                    reads=[wo_b[ws], oT_b[c][ti]], writes=[bankb[bk]], inc=(c == 1))
                            S.op("dve", lambda dc=dc, c0=c0, n=n, bk=bk: nc.vector.tensor_tensor(
                                out=hT[:, dc, c0:c0 + n], in0=hT[:, dc, c0:c0 + n], in1=banks[bk][:, 0:n], op=ALU.add),
                                reads=[bankb[bk], h_b[dc][ti]], writes=[h_b[dc][ti]])
                S.barrier()

        def gla_mixer(j, gidx):
            with ExitStack() as ph:
                wqkv_t = sb("g_wqkv", [128, 8, 512], BF16, ph)
                wr_t = sb("g_wr", [128, 8, 256], BF16, ph)
                wo_t = sb("g_wo", [128, 2, 1024], BF16, ph)
                wgl = sb("g_wgl", [128, 8, 16], BF16, ph)
                wga = sb("g_wga", [32, 512], BF16, ph)
                glT = sb("g_glT", [32, T], BF16, ph)
                qd = sb("g_qd", [128, T], BF16, ph)
                kd = sb("g_kd", [128, T], BF16, ph)
                kst = sb("g_kst", [128, 17, 128], BF16, ph)
                vt = sb("g_vt", [128, 17, 256], BF16, ph)
                Eb = sb("g_Eb", [128, T], F32, ph)
                Enb = sb("g_Enb", [128, T], F32, ph)
                Eaft = sb("g_Eaft", [128, 17, 128], F32, ph)
                et = [sb("g_e%d" % i, [128, 128], F32, ph) for i in range(2)]
                spt = [sb("g_sp%d" % i, [128, 128], F32, ph) for i in range(2)]
                shi = [sb("g_shi%d" % i, [128, 128], BF16, ph) for i in range(2)]
                slo = [sb("g_slo%d" % i, [128, 128], BF16, ph) for i in range(2)]
                shi_b = [Buf(), Buf()]
                slo_b = [Buf(), Buf()]
                oTt = sb("g_oT", [128, 2, T], F32, ph)
                rstd = Enb
                tmp = [sb("g_tmp", [128, 512], F32, ph)] * 2
                sq = [sb("g_sq", [128, 2, 512], BF16, ph)] * 2
                sr = [sb("g_sr", [128, 2, 512], BF16, ph)] * 2
                t1 = [sb("g_t1", [128, 2, 512], F32, ph)] * 2
                og = [sb("g_og%d" % i, [128, 2, 512], BF16, ph) for i in range(2)]
                S32 = sb("g_S32", [128, 256], F32, ph)
                Sbf2 = [sb("g_Sbf%d" % i, [128, 256], BF16, ph) for i in range(2)]
                attm = [sb("g_att%d" % i, [128, 128], BF16, ph) for i in range(2)]
                wqkv_b, wr_b, wo_b = Buf(), Buf(), Buf()
                wqkv_s, wr_s, wo_s = S.dma_sem(), S.dma_sem(), S.dma_sem()
                wg_s = S.dma_sem()
                wg_b = Buf()
                glT_b = [Buf() for _ in range(5)]
                qd_b = [Buf() for _ in range(5)]
                kd_b = [Buf() for _ in range(5)]
                kst_b = [Buf() for _ in range(17)]
                vt_b = [Buf() for _ in range(17)]
                Eb_b = [Buf() for _ in range(17)]
                Enb_b = [Buf() for _ in range(17)]
                Eaft_b = [Buf() for _ in range(17)]
                et_b = [Buf(), Buf()]
                sp_b = [Buf(), Buf()]
                oT_b = [Buf() for _ in range(17)]
                rstd_b = [Buf() for _ in range(5)]
                tmp_b = [Buf()] * 2
                sq_b = [Buf()] * 2
                sr_b = [Buf()] * 2
                t1_b = [Buf()] * 2
                og_b = [Buf(), Buf()]
                S32_b = Buf()
                Sbf2_b = [Buf(), Buf()]
                att_b = [Buf(), Buf()]
                rmsnorm(gidx, [sq[0][:, 0, :], sq[0][:, 1, :]], [Buf(), Buf()], [tmp[0]], [tmp_b[0]],
                        [t1[0][:, 0, :], t1[0][:, 1, :]], [Buf(), Buf()])
                S.barrier()

                def load_qkv(hd):
                    S.dma("pool", wqkv_t[:].rearrange("p a b -> p (a b)"), win_d[j, hd], wqkv_s, writes=[wqkv_b])

                def load_ro(hd):
                    S.dma("pool", wr_t[:].rearrange("p a b -> p (a b)"), winr_d[j, hd], wr_s, writes=[wr_b])
                    S.dma("pool", wo_t[:].rearrange("p a b -> p (a b)"), wogl_d[j, hd], wo_s, writes=[wo_b])

                S.dma("pool", wgl[:].rearrange("p a b -> p (a b)"), wgl_d[j], wg_s, writes=[wg_b])
                S.op("pool", lambda: nc.gpsimd.memset(wga[:], 0.0), writes=[wg_b])
                S.dma("pool", wga[0:17, :], wga_d[j], wg_s, writes=[wg_b])
                load_qkv(0)
                load_ro(0)
                S.op("dve", lambda: nc.vector.memset(glT[:], 1.0), writes=glT_b)
                pcount = 0
                for ti, (c0, n) in enumerate(TT):
                    p = pcount % 2
                    pcount += 1
                    bk = 6 + p
                    for kc in range(8):
                        S.op("pe", lambda kc=kc, c0=c0, n=n, bk=bk: nc.tensor.matmul(
                            banks[bk][0:16, 0:n], wgl[:, kc, :], yn[:, kc, c0:c0 + n], start=(kc == 0), stop=(kc == 7)),
                            reads=[wg_b, yn_b[ti]], writes=[bankb[bk]], inc=(kc == 7))
                    S.op("dve", lambda c0=c0, n=n, bk=bk: nc.vector.tensor_copy(
                        out=glT[0:16, c0:c0 + n], in_=banks[bk][0:16, 0:n]),
                        reads=[bankb[bk]], writes=[glT_b[ti]])

                for hd in range(4):
                    for gi, (c0g, ng) in enumerate(TT):
                        tiles = [16] if gi == 4 else [4 * gi + u for u in range(4)]
                        L = len(tiles)
                        n = TOKT[tiles[0]][1]
                        W = L * 128 if n == 128 else n
                        p = gi % 2
                        b0, b1, b2 = 0 + p, 2 + p, 4 + p
                        spv = t1[0][:, p, :]
                        hi = sq[0][:, 0, :] if p == 0 else og[0][:, 0, :]
                        lo = sq[0][:, 1, :] if p == 0 else og[0][:, 1, :]
                        hl_b = sq_b[0] if p == 0 else og_b[0]
                        for u, ti_ in enumerate(tiles):
                            c0 = TOKT[ti_][0]
                            S.op("pe", lambda c0=c0, n=n, b0=b0, u=u: nc.tensor.matmul(
                                banks[b0][0:n, u * 128:(u + 1) * 128], glT[0:32, c0:c0 + n],
                                wga[0:32, hd * 128:(hd + 1) * 128], start=True, stop=True),
                                reads=[glT_b[gi], wg_b], writes=[bankb[b0]], inc=(u == L - 1))
                        Wg = L * 128
                        S.op("act", lambda n=n, b0=b0, Wg=Wg: nc.scalar.activation(
                            out=tmp[0][0:n, 0:Wg], in_=banks[b0][0:n, 0:Wg], func=AF.Exp, scale=-1.0),
                            reads=[bankb[b0]], writes=[tmp_b[0]])
                        S.op("act", lambda n=n, Wg=Wg, spv=spv: nc.scalar.activation(
                            out=spv[0:n, 0:Wg], in_=tmp[0][0:n, 0:Wg], func=AF.Ln, bias=1.0, scale=1.0),
                            reads=[tmp_b[0]], writes=[t1_b[0]])
                        S.op("dve", lambda n=n, Wg=Wg, spv=spv, hi=hi: nc.vector.tensor_copy(
                            out=hi[0:n, 0:Wg], in_=spv[0:n, 0:Wg]), reads=[t1_b[0]], writes=[hl_b])
                        S.op("dve", lambda n=n, Wg=Wg, spv=spv, hi=hi, lo=lo: nc.vector.tensor_tensor(
                            out=lo[0:n, 0:Wg], in0=spv[0:n, 0:Wg], in1=hi[0:n, 0:Wg], op=ALU.subtract),
                            reads=[t1_b[0], hl_b], writes=[hl_b])
                        for u in range(L):
                            for part, src in enumerate((hi, lo)):
                                S.op("pe", lambda n=n, b1=b1, u=u, src=src, part=part: nc.tensor.matmul(
                                    banks[b1][:, u * 128:u * 128 + n], src[0:n, u * 128:(u + 1) * 128], TIs[0:n, 0:n],
                                    start=(part == 0), stop=(part == 1)),
                                    reads=[hl_b, constb], writes=[bankb[b1]], inc=(u == L - 1 and part == 1))
                        for u in range(L):
                            for part, src in enumerate((hi, lo)):
                                S.op("pe", lambda n=n, b2=b2, u=u, src=src, part=part: nc.tensor.matmul(
                                    banks[b2][0:n, u * 128:(u + 1) * 128], SUs[0:n, 0:n], src[0:n, u * 128:(u + 1) * 128],
                                    start=(part == 0), stop=(part == 1)),
                                    reads=[hl_b, constb], writes=[bankb[b2]], inc=(u == L - 1 and part == 1))
                        S.op("act", lambda c0g=c0g, ng=ng, b1=b1: nc.scalar.activation(
                            out=Eb[:, c0g:c0g + ng], in_=banks[b1][:, 0:ng], func=AF.Exp),
                            reads=[bankb[b1]], writes=[Eb_b[u_] for u_ in tiles])
                        S.op("act", lambda c0g=c0g, ng=ng, b1=b1: nc.scalar.activation(
                            out=Enb[:, c0g:c0g + ng], in_=banks[b1][:, 0:ng], func=AF.Exp, scale=-1.0),
                            reads=[bankb[b1]], writes=[Enb_b[u_] for u_ in tiles] + [rstd_b[gi]])
                        t0_ = tiles[0]
                        S.op("act", lambda n=n, b2=b2, Wg=Wg, t0_=t0_, L=L: nc.scalar.activation(
                            out=Eaft[0:n, t0_:t0_ + L, :],
                            in_=banks[b2][0:n, 0:Wg].rearrange("p (l k) -> p l k", l=L), func=AF.Exp),
                            reads=[bankb[b2]], writes=[Eaft_b[u_] for u_ in tiles])
                    if gla_steps < 2:
                        continue
                    for ti, (c0, n) in enumerate(TT):
                        tiles_in = [16] if ti == 4 else [4 * ti + u for u in range(4)]
                        r4 = pcount % 4
                        pcount += 1
                        bq, bkk = 2 * r4, 2 * r4 + 1
                        for kc in range(8):
                            S.op("pe", lambda kc=kc, c0=c0, n=n, bq=bq: nc.tensor.matmul(
                                banks[bq][:, 0:n], wqkv_t[:, kc, 0:128], yn[:, kc, c0:c0 + n],
                                start=(kc == 0), stop=(kc == 7)),
                                reads=[wqkv_b, yn_b[ti]], writes=[bankb[bq]], inc=(kc == 7))
                        for kc in range(8):
                            S.op("pe", lambda kc=kc, c0=c0, n=n, bkk=bkk: nc.tensor.matmul(
                                banks[bkk][:, 0:n], wqkv_t[:, kc, 128:256], yn[:, kc, c0:c0 + n],
                                start=(kc == 0), stop=(kc == 7)),
                                reads=[wqkv_b, yn_b[ti]], writes=[bankb[bkk]], inc=(kc == 7))
                        S.op("dve", lambda c0=c0, n=n, bq=bq: nc.vector.scalar_tensor_tensor(
                            out=qd[:, c0:c0 + n], in0=banks[bq][:, 0:n], scalar=128.0 ** -0.5, in1=Eb[:, c0:c0 + n],
                            op0=ALU.mult, op1=ALU.mult),
                            reads=[bankb[bq]] + [Eb_b[u] for u in tiles_in], writes=[qd_b[ti]])
                        S.op("dve", lambda c0=c0, n=n, bkk=bkk: nc.vector.tensor_tensor(
                            out=kd[:, c0:c0 + n], in0=Enb[:, c0:c0 + n], in1=banks[bkk][:, 0:n], op=ALU.mult),
                            reads=[bankb[bkk]] + [Enb_b[u] for u in tiles_in], writes=[kd_b[ti]])
                    for i, (c0, n) in enumerate(TOKT if gla_steps >= 2.5 else []):
                        bk = pcount % 8
                        pcount += 1
                        for kc in range(8):
                            S.op("pe", lambda kc=kc, c0=c0, n=n, bk=bk: nc.tensor.matmul(
                                banks[bk][0:n, 0:384], yn[:, kc, c0:c0 + n], wqkv_t[:, kc, 128:512],
                                start=(kc == 0), stop=(kc == 7)),
                                reads=[wqkv_b, yn_b[tt_of_tok(i)]], writes=[bankb[bk]], inc=(kc == 7))
                        S.op("dve", lambda i=i, n=n, bk=bk: nc.vector.tensor_tensor(
                            out=kst[0:n, i, :], in0=Eaft[0:n, i, :], in1=banks[bk][0:n, 0:128], op=ALU.mult),
                            reads=[bankb[bk], Eaft_b[i]], writes=[kst_b[i]])
                        S.op("act", lambda i=i, n=n, bk=bk: nc.scalar.copy(out=vt[0:n, i, :], in_=banks[bk][0:n, 128:384]),
                             reads=[bankb[bk], kst_b[i]], writes=[vt_b[i]])
                    if gla_steps < 3:
                        continue
                    if hd + 1 < 4:
                        load_qkv(hd + 1)
                    S.op("dve", lambda: nc.vector.memset(S32[:], 0.0), writes=[S32_b])
                    order = [16] + list(range(16))

                    def c_att(oi):
                        i = order[oi]
                        c0, n = TOKT[i]
                        ti = tt_of_tok(i)
                        p = oi % 2
                        ba = 0 + p
                        S.op("pe", lambda: nc.tensor.matmul(
                            banks[ba][0:n, 0:n], kd[:, c0:c0 + n], qd[:, c0:c0 + n], start=True, stop=True),
                            reads=[kd_b[ti], qd_b[ti]], writes=[bankb[ba]])
                        S.op("dve", lambda: nc.vector.tensor_tensor(
                            out=attm[p][0:n, 0:n], in0=mask01[0:n, 0:n], in1=banks[ba][0:n, 0:n], op=ALU.mult),
                            reads=[bankb[ba], constb], writes=[att_b[p]])

                    def c_kv(oi):
                        i = order[oi]
                        c0, n = TOKT[i]
                        bs = 4 + (oi % 4)
                        S.op("pe", lambda: nc.tensor.matmul(
                            banks[bs][:, 0:256], kst[0:n, i, :], vt[0:n, i, :], start=True, stop=True),
                            reads=[kst_b[i], vt_b[i]], writes=[bankb[bs]])

                    c_att(0)
                    c_kv(0)
                    for oi, i in enumerate(order):
                        c0, n = TOKT[i]
                        ti = tt_of_tok(i)
                        p = oi % 2
                        bo, bs = 2 + p, 4 + (oi % 4)
                        Sprev, Sprev_b = Sbf2[(oi + 1) % 2], Sbf2_b[(oi + 1) % 2]
                        Snew, Snew_b = Sbf2[oi % 2], Sbf2_b[oi % 2]
                        if oi + 1 < len(order):
                            c_att(oi + 1)
                            c_kv(oi + 1)
                        for c in range(2):
                            S.op("pe", lambda c=c, i=i, p=p, n=n, bo=bo: nc.tensor.matmul(
                                banks[bo][:, c * 128:c * 128 + n], vt[0:n, i, c * 128:(c + 1) * 128], attm[p][0:n, 0:n],
                                start=True, stop=(oi == 0)),
                                reads=[vt_b[i], att_b[p]], writes=[bankb[bo]], inc=(oi == 0 and c == 1))
                            if oi > 0:
                                S.op("pe", lambda c=c, c0=c0, n=n, bo=bo: nc.tensor.matmul(
                                    banks[bo][:, c * 128:c * 128 + n], Sprev[:, c * 128:(c + 1) * 128], qd[:, c0:c0 + n],
                                    start=False, stop=True),
                                    reads=[Sprev_b, qd_b[ti]], writes=[bankb[bo]], inc=(c == 1))
                        src_ap = banks[bo][:, 0:256].rearrange("p (c q) -> p c q", c=2)[:, :, 0:n]
                        S.op("act", lambda src_ap=src_ap, c0=c0, n=n: nc.scalar.copy(
                            out=oTt[:, :, c0:c0 + n], in_=src_ap),
                            reads=[bankb[bo]], writes=[oT_b[i]])
                        S.op("dve", lambda c0=c0, n=n, bs=bs: nc.vector.scalar_tensor_tensor(
                            out=Snew[:], in0=S32[:], scalar=Eb[:, c0 + n - 1:c0 + n], in1=banks[bs][:, 0:256],
                            op0=ALU.mult, op1=ALU.add),
                            reads=[S32_b, Eb_b[i], bankb[bs]], writes=[Snew_b])
                        S.op("dve", lambda c0=c0, n=n, bs=bs: nc.vector.scalar_tensor_tensor(
                            out=S32[:], in0=S32[:], scalar=Eb[:, c0 + n - 1:c0 + n], in1=banks[bs][:, 0:256],
                            op0=ALU.mult, op1=ALU.add),
                            reads=[S32_b, Eb_b[i], bankb[bs]], writes=[S32_b])
                    if gla_steps < 4:
                        continue
                    for ti, (c0, n) in enumerate(TT):
                        tiles_in = [16] if ti == 4 else [4 * ti + u for u in range(4)]
                        p = ti % 2
                        bk = 6 + p
                        S.op("act", lambda p=p, c0=c0, n=n: nc.scalar.activation(
                            out=sq[p][:, :, 0:n], in_=oTt[:, :, c0:c0 + n], func=AF.Square),
                            reads=[oT_b[u] for u in tiles_in], writes=[sq_b[p]])
                        for c in range(2):
                            S.op("pe", lambda c=c, p=p, n=n, bk=bk: nc.tensor.matmul(
                                banks[bk][:, 0:n], ones256, sq[p][:, c, 0:n], start=(c == 0), stop=(c == 1)),
                                reads=[sq_b[p], constb], writes=[bankb[bk]], inc=(c == 1))
                        S.op("act", lambda p=p, n=n, bk=bk: nc.scalar.activation(
                            out=tmp[p][:, 0:n], in_=banks[bk][:, 0:n], func=AF.Ln, bias=epsT[:, 0:1], scale=1.0),
                            reads=[bankb[bk], constb], writes=[tmp_b[p]])
                        S.op("act", lambda p=p, n=n, c0=c0: nc.scalar.activation(
                            out=rstd[:, c0:c0 + n], in_=tmp[p][:, 0:n], func=AF.Exp, scale=-0.5),
                            reads=[tmp_b[p]], writes=[rstd_b[ti]] + [Enb_b[u] for u in tiles_in])
                    def d_rproj(ti):
                        c0, n = TT[ti]
                        for c in range(2):
                            bk = (2 * ti + c) % 4
                            for kc in range(8):
                                S.op("pe", lambda kc=kc: nc.tensor.matmul(
                                    banks[bk][:, 0:n], wr_t[:, kc, c * 128:(c + 1) * 128],
                                    yn[:, kc, c0:c0 + n], start=(kc == 0), stop=(kc == 7)),
                                    reads=[wr_b, yn_b[ti]], writes=[bankb[bk]], inc=(kc == 7))

                    def d_rest(ti):
                        c0, n = TT[ti]
                        tiles_in = [16] if ti == 4 else [4 * ti + u for u in range(4)]
                        p = ti % 2
                        for c in range(2):
                            bk = (2 * ti + c) % 4
                            S.op("act", lambda c=c, bk=bk: nc.scalar.activation(
                                out=sr[p][:, c, 0:n], in_=banks[bk][:, 0:n], func=AF.Silu),
                                reads=[bankb[bk]], writes=[sr_b[p]])
                        for c in range(2):
                            S.op("dve", lambda c=c: nc.vector.scalar_tensor_tensor(
                                out=t1[p][:, c, 0:n], in0=oTt[:, c, c0:c0 + n],
                                scalar=gon[:, j * 8 + hd * 2 + c:j * 8 + hd * 2 + c + 1], in1=rstd[:, c0:c0 + n],
                                op0=ALU.mult, op1=ALU.mult),
                                reads=[oT_b[u] for u in tiles_in] + [rstd_b[ti], constb], writes=[t1_b[p]])
                        S.op("pool", lambda: nc.gpsimd.tensor_tensor(
                            out=og[p][:, :, 0:n], in0=t1[p][:, :, 0:n], in1=sr[p][:, :, 0:n], op=ALU.mult),
                            reads=[t1_b[p], sr_b[p]], writes=[og_b[p]])
                        for dc in range(8):
                            bk = 4 + (dc % 4)
                            for c in range(2):
                                S.op("pe", lambda c=c, dc=dc, bk=bk: nc.tensor.matmul(
                                    banks[bk][:, 0:n], wo_t[:, c, dc * 128:(dc + 1) * 128], og[p][:, c, 0:n],
                                    start=(c == 0), stop=(c == 1)),
                                    reads=[wo_b, og_b[p]], writes=[bankb[bk]], inc=(c == 1))
                            S.op("dve", lambda dc=dc, bk=bk: nc.vector.tensor_tensor(
                                out=hT[:, dc, c0:c0 + n], in0=hT[:, dc, c0:c0 + n], in1=banks[bk][:, 0:n], op=ALU.add),
                                reads=[bankb[bk], h_b[dc][ti]], writes=[h_b[dc][ti]])

                    d_rproj(0)
                    for ti in range(len(TT)):
                        if ti + 1 < len(TT):
                            d_rproj(ti + 1)
                        d_rest(ti)
                    if hd + 1 < 4:
                        load_ro(hd + 1)
                S.barrier()

        sub = 0
        if dbg == "gla":
            gla_mixer(0, 4)
            n_sub = 0
        for l in range(DEPTH):
            for part in range(3):
                if sub >= n_sub:
                    break
                if part == 0:
                    ffn(2 * l, 3 * l)
                elif part == 1:
                    if l % 2 == 0:
                        sb_mixer(l // 2, 3 * l + 1)
                    else:
                        gla_mixer(l // 2, 3 * l + 1)
                else:
                    ffn(2 * l + 1, 3 * l + 2)
                sub += 1

        with ExitStack() as ph:
            ys = [sb("ys%d" % i, [128, D], F32, ph) for i in range(2)]
            ys_b = [Buf(), Buf()]
            for i in range(16):
                s = i % 2
                c0 = i * 128
                for half in range(2):
                    bk = (2 * i + half) % 4
                    for q in range(4):
                        kc = half * 4 + q
                        S.op("pe", lambda bk=bk, q=q, kc=kc, c0=c0: nc.tensor.transpose(
                            out=banks[bk][:, q * 128:(q + 1) * 128], in_=hT[:, kc, c0:c0 + 128], identity=ident),
                            reads=[h_b[kc][i // 4], constb], writes=[bankb[bk]], inc=(q == 3))
                    if half == 0:
                        S.op("act", lambda bk=bk, s=s, half=half: nc.scalar.copy(
                            out=ys[s][:, half * 512:(half + 1) * 512], in_=banks[bk][:, :]),
                            reads=[bankb[bk]], writes=[ys_b[s]])
                    else:
                        S.op("dve", lambda bk=bk, s=s, half=half: nc.vector.tensor_copy(
                            out=ys[s][:, half * 512:(half + 1) * 512], in_=banks[bk][:, :]),
                            reads=[bankb[bk]], writes=[ys_b[s]])
                S.dma("sp", y_d[c0:c0 + 128, :], ys[s][:], d_out[s], reads=[ys_b[s]])
            S.barrier()
    return nc


def _consts():
    i = np.arange(128)
    cf = np.zeros((128, 4, 128), np.float32)
    cf[:, 0] = np.eye(128, dtype=np.float32)
    cf[:, 1] = -(1.0 / 16.0) * (i[:, None] <= i[None, :])
    cf[:, 2] = -(1.0 / 16.0) * (i[:, None] > i[None, :])
    cf[:, 3] = (i[:, None] <= i[None, :])
    cb = np.zeros((128, 9, 128), np.float32)
    cb[:, 0] = 1.0 / 1024.0
    cb[:, 1] = (i[:, None] // 64 == i[None, :] // 64) * (1.0 / 64.0)
    cb[:, 2] = 1.0 / 256.0
    cb[:, 3] = -1.0 * (i[:, None] >= i[None, :])
    cb[:, 4] = -1.0
    cb[:, 5] = np.eye(128, dtype=np.float32)
    cb[:, 6] = np.where(i[:, None] < i[None, :], 0.0, NEG)
    cb[:, 7] = cf[:, 1]
    cb[:, 8] = cf[:, 2]
    return cf.reshape(128, 512), cb.reshape(128, 1152)


def _layout(inp):
    f = lambda a: np.ascontiguousarray(np.asarray(a, dtype=np.float32))
    out = {}
    wgu = np.empty((2 * DEPTH, NFC, 128, 8, 256), np.float32)
    wd = np.empty((2 * DEPTH, 2, 8, 128, 11, 128), np.float32)
    for l in range(DEPTH):
        for wh, (kgu, kd) in enumerate((("ffn_a_w_gu", "ffn_a_w_down"), ("ffn_b_w_gu", "ffn_b_w_down"))):
            w = f(inp[kgu][l]).reshape(8, 128, 2, NFC, 128)
            wgu[2 * l + wh] = w.transpose(3, 1, 0, 2, 4).reshape(NFC, 128, 8, 256)
            w2 = f(inp[kd][l]).reshape(2, 11, 128, 8, 128)
            wd[2 * l + wh] = w2.transpose(0, 3, 2, 1, 4)
    out["wgu"] = wgu.reshape(2 * DEPTH, NFC, 128, 8 * 256)
    out["wd"] = wd.reshape(2 * DEPTH, 2, 8, 128, 11 * 128)
    wqkv = f(inp["sb_w_qkv"]).reshape(2, 8, 128, 3, 4, 256)
    out["wqkv"] = np.ascontiguousarray(wqkv.transpose(0, 4, 2, 1, 3, 5)).reshape(2, 4, 128, 8 * 768)
    wosb = f(inp["sb_w_o"]).reshape(2, 4, 2, 128, 1024)
    out["wosb"] = np.ascontiguousarray(wosb.transpose(0, 1, 3, 2, 4)).reshape(2, 4, 128, 2048)
    w_in = f(inp["gla_w_in"])
    win = np.empty((2, 4, 128, 8, 768), np.float32)
    for hd in range(4):
        cols = np.concatenate([np.arange(hd * 128, hd * 128 + 128), 512 + np.arange(hd * 128, hd * 128 + 128),
                               1024 + np.arange(hd * 256, hd * 256 + 256), 2048 + np.arange(hd * 256, hd * 256 + 256)])
        win[:, hd] = w_in[:, :, cols].reshape(2, 8, 128, 768).transpose(0, 2, 1, 3)
    out["win"] = np.ascontiguousarray(win[:, :, :, :, 0:512]).reshape(2, 4, 128, 8 * 512)
    out["winr"] = np.ascontiguousarray(win[:, :, :, :, 512:768]).reshape(2, 4, 128, 8 * 256)
    out["wgl"] = np.ascontiguousarray(w_in[:, :, 3072:3088].reshape(2, 8, 128, 16).transpose(0, 2, 1, 3)).reshape(2, 128, 128)
    out["wga"] = np.ascontiguousarray(np.concatenate([f(inp["gla_w_gate_up"]), f(inp["gla_b_gate"])[:, None, :]], axis=1))
    wogl = f(inp["gla_w_o"]).reshape(2, 4, 2, 128, 1024)
    out["wogl"] = np.ascontiguousarray(wogl.transpose(0, 1, 3, 2, 4)).reshape(2, 4, 128, 2048)
    norms = np.stack([f(inp["ffn_a_norm"]), f(inp["mix_norm"]), f(inp["ffn_b_norm"])], axis=1)
    out["norms"] = np.ascontiguousarray(norms.reshape(12, 8, 128).transpose(2, 0, 1)).reshape(128, 96)
    out["gon"] = np.ascontiguousarray(f(inp["gla_out_norm"]).reshape(2, 8, 128).transpose(2, 0, 1)).reshape(128, 16)
    gq = f(inp["sb_q_norm"])
    gk = f(inp["sb_k_norm"])
    gqk = np.stack([gq[0], gk[0], gq[1], gk[1]], axis=1)
    out["gqk"] = np.ascontiguousarray(np.concatenate([gqk, gqk], axis=0))
    out["meta"] = f(inp["meta"])
    cf, cb = _consts()
    out["cf"] = cf
    out["cb"] = cb
    return out


_NC_CACHE = {}


def kernel(**inputs):
    shared = _layout(inputs)
    x = np.ascontiguousarray(np.asarray(inputs["x"], dtype=np.float32))
    if "nc" not in _NC_CACHE:
        _NC_CACHE["nc"] = build_program()
    nc = _NC_CACHE["nc"]
    in_maps = []
    for b in range(8):
        m = dict(shared)
        m["x"] = x[b]
        in_maps.append(m)
    res = run_bass_kernel_spmd(nc, in_maps, core_ids=list(range(8)))
    return np.stack([np.asarray(r["y"], dtype=np.float32) for r in res.results], axis=0)
```
